# Optimizing a Trainium2 kernel written in Bass

```python
import jax, jax.numpy as jnp
from jax import lax
import numpy as np

D_MODEL = 2048
BATCH = 4
SEQ = 4096
DEPTH = 1

ATTN_HEADS = 16
ATTN_HEAD_DIM = 128
ATTN_WIDTH = ATTN_HEADS * ATTN_HEAD_DIM
DILATED_PATTERNS = ((128, 1), (512, 4), (2048, 16))
BAND_BLOCK = 128
MLSTM_HEADS = 8
MLSTM_QK_DIM = 128
MLSTM_V_DIM = 256
MLSTM_QK_WIDTH = MLSTM_HEADS * MLSTM_QK_DIM
MLSTM_V_WIDTH = MLSTM_HEADS * MLSTM_V_DIM
MLSTM_CHUNK = 64
CONV_WIDTH = 4
NORM_EPS = 1e-6

IN_SPLIT_SIZES = (
    ATTN_WIDTH, ATTN_WIDTH, ATTN_WIDTH,
    ATTN_WIDTH,
    2 * MLSTM_QK_WIDTH,
    MLSTM_V_WIDTH,
    MLSTM_HEADS, MLSTM_HEADS,
    MLSTM_V_WIDTH,
    MLSTM_V_WIDTH,
    D_MODEL, D_MODEL,
)
IN_WIDTH = int(sum(IN_SPLIT_SIZES))
IN_SPLIT_POINTS = [int(v) for v in np.cumsum(IN_SPLIT_SIZES)[:-1]]

kernel_name = "hybrid_dilated_attn_mlstm_gated_block"


def _rmsnorm(x, g):
    xf = x.astype(jnp.float32)
    y = xf * lax.rsqrt(jnp.mean(xf * xf, axis=-1, keepdims=True) + NORM_EPS)
    return (y * g.astype(jnp.float32)).astype(x.dtype)


def _causal_conv_silu(u, w, b):
    k_taps = w.shape[0]
    s = u.shape[1]
    up = jnp.pad(u, ((0, 0), (k_taps - 1, 0), (0, 0)))
    y = b
    for j in range(k_taps):
        y = y + up[:, j:j + s] * w[j]
    return jax.nn.silu(y)


def _dilated_band(q, k, v, slopes, window, dilation):
    bsz, s, h, dh = q.shape
    n_back = window // dilation
    span = dilation * BAND_BLOCK
    s_pad = -(-s // span) * span
    n_blk = s_pad // span
    length = s_pad // dilation

    def phase(a):
        a = jnp.pad(a, ((0, 0), (0, s_pad - s), (0, 0), (0, 0)))
        a = a.reshape(bsz, length, dilation, h, dh).transpose(0, 2, 1, 3, 4)
        return a.reshape(bsz, dilation, n_blk, BAND_BLOCK, h, dh)

    def with_prev(a):
        prev = jnp.pad(a, ((0, 0), (0, 0), (1, 0), (0, 0), (0, 0), (0, 0)))[:, :, :-1]
        return jnp.concatenate([prev, a], axis=3)

    qb = phase(q)
    kk = with_prev(phase(k))
    vv = with_prev(phase(v))
    scores = jnp.einsum('brnqhd,brnkhd->brnhqk', qb, kk,
                        preferred_element_type=jnp.float32)
    qi = jnp.arange(BAND_BLOCK)
    ki = jnp.arange(2 * BAND_BLOCK)
    dist = BAND_BLOCK + qi[:, None] - ki[None, :]
    key_pos = (jnp.arange(n_blk)[:, None] - 1) * BAND_BLOCK + ki[None, :]
    valid = (dist >= 0)[None] & (dist <= n_back)[None] & (key_pos >= 0)[:, None, :]
    alibi = -slopes[:, None, None] * (dist * dilation).astype(jnp.float32)[None]
    scores = jnp.where(valid[None, None, :, None], scores + alibi, -jnp.inf)
    mx = jnp.max(scores, axis=-1)
    p = jnp.exp(scores - mx[..., None])
    den = jnp.sum(p, axis=-1)
    num = jnp.einsum('brnhqk,brnkhd->brnqhd', p, vv.astype(jnp.float32))

    num = num.reshape(bsz, dilation, length, h, dh).transpose(0, 2, 1, 3, 4)
    num = num.reshape(bsz, s_pad, h, dh)[:, :s]

    def unphase_stat(a):
        a = a.transpose(0, 1, 2, 4, 3).reshape(bsz, dilation, length, h)
        return a.transpose(0, 2, 1, 3).reshape(bsz, s_pad, h)[:, :s]

    return num, unphase_stat(den), unphase_stat(mx)


def _dilated_attention(q, k, v):
    q = q * (ATTN_HEAD_DIM ** -0.5)
    slopes = jnp.exp2(-8.0 * jnp.arange(1, ATTN_HEADS + 1, dtype=jnp.float32) / ATTN_HEADS)
    nums, dens, mxs = [], [], []
    for window, dilation in DILATED_PATTERNS:
        n_, d_, m_ = _dilated_band(q, k, v, slopes, window, dilation)
        nums.append(n_); dens.append(d_); mxs.append(m_)
    m_all = jnp.max(jnp.stack(mxs, 0), axis=0)
    wts = [jnp.exp(m_ - m_all) for m_ in mxs]
    num = sum(w_[..., None] * n_ for w_, n_ in zip(wts, nums))
    den = sum(w_ * d_ for w_, d_ in zip(wts, dens))
    return num / den[..., None]


def _mlstm_chunkwise(q, k, v, ig, lf):
    bsz, s, h, dk = q.shape
    dv = v.shape[-1]
    nc = s // MLSTM_CHUNK
    L = MLSTM_CHUNK

    def chunks(a):
        a = a.reshape((bsz, nc, L, h) + a.shape[3:])
        perm = (1, 0, 3, 2) + tuple(range(4, a.ndim))
        return a.transpose(perm)

    xs = (chunks(q), chunks(k * (dk ** -0.5)), chunks(v), chunks(ig), chunks(lf))
    causal = jnp.tril(jnp.ones((L, L), dtype=bool))

    def step(carry, inp):
        c_st, n_st, m_st = carry
        qc, kc, vc, igc, lfc = inp
        b = jnp.cumsum(lfc, axis=-1)
        a = b + m_st[..., None]
        dmat = b[..., :, None] - b[..., None, :] + igc[..., None, :]
        dmat = jnp.where(causal, dmat, -jnp.inf)
        m_t = jnp.maximum(a, jnp.max(dmat, axis=-1))
        w_intra = jnp.exp(dmat - m_t[..., None])
        w_inter = jnp.exp(a - m_t)
        sc = jnp.einsum('bhtd,bhsd->bhts', qc, kc) * w_intra
        num = (w_inter[..., None] * jnp.einsum('bhtd,bhde->bhte', qc, c_st)
               + jnp.einsum('bhts,bhse->bhte', sc, vc))
        den = w_inter * jnp.einsum('bhtd,bhd->bht', qc, n_st) + jnp.sum(sc, axis=-1)
        h_out = num / jnp.maximum(jnp.abs(den), jnp.exp(-m_t))[..., None]
        b_last = b[..., -1]
        g = b_last[..., None] - b + igc
        m_new = jnp.maximum(b_last + m_st, jnp.max(g, axis=-1))
        w_old = jnp.exp(b_last + m_st - m_new)
        w_s = jnp.exp(g - m_new[..., None])
        c_new = w_old[..., None, None] * c_st + jnp.einsum('bhs,bhsd,bhse->bhde', w_s, kc, vc)
        n_new = w_old[..., None] * n_st + jnp.einsum('bhs,bhsd->bhd', w_s, kc)
        return (c_new, n_new, m_new), h_out

    init = (jnp.zeros((bsz, h, dk, dv), jnp.float32),
            jnp.zeros((bsz, h, dk), jnp.float32),
            jnp.zeros((bsz, h), jnp.float32))
    _, hs = lax.scan(step, init, xs)
    return hs.transpose(1, 0, 3, 2, 4).reshape(bsz, s, h, dv)


def setup_inputs(seed: int = 0) -> dict:
    key = jax.random.key(seed)
    ks = jax.random.split(key, 14)
    f32 = jnp.float32
    x = jax.random.normal(ks[0], (BATCH, SEQ, D_MODEL), f32)
    norm_g = 1.0 + 0.02 * jax.random.normal(ks[1], (D_MODEL,), f32)
    w_in = jax.random.normal(ks[2], (D_MODEL, IN_WIDTH), f32) * D_MODEL ** -0.5
    b_i = 0.1 * jax.random.normal(ks[3], (MLSTM_HEADS,), f32)
    b_f = jnp.linspace(3.0, 6.0, MLSTM_HEADS, dtype=f32) + 0.1 * jax.random.normal(ks[4], (MLSTM_HEADS,), f32)
    b_if = jnp.concatenate([b_i, b_f])
    conv_w = jax.random.normal(ks[5], (CONV_WIDTH, 2 * MLSTM_QK_WIDTH), f32) * CONV_WIDTH ** -0.5
    conv_b = 0.02 * jax.random.normal(ks[6], (2 * MLSTM_QK_WIDTH,), f32)
    mlstm_norm_g = 1.0 + 0.02 * jax.random.normal(ks[7], (MLSTM_V_WIDTH,), f32)
    w_attn_branch = jax.random.normal(ks[8], (ATTN_WIDTH, D_MODEL), f32) * ATTN_WIDTH ** -0.5
    w_mlstm_branch = jax.random.normal(ks[9], (MLSTM_V_WIDTH, D_MODEL), f32) * MLSTM_V_WIDTH ** -0.5
    w_out = jax.random.normal(ks[10], (D_MODEL, D_MODEL), f32) * D_MODEL ** -0.5
    final_norm_g = 1.0 + 0.02 * jax.random.normal(ks[11], (D_MODEL,), f32)
    return {"x": x, "norm_g": norm_g, "w_in": w_in, "b_if": b_if, "conv_w": conv_w,
            "conv_b": conv_b, "mlstm_norm_g": mlstm_norm_g, "w_attn_branch": w_attn_branch,
            "w_mlstm_branch": w_mlstm_branch, "w_out": w_out, "final_norm_g": final_norm_g}


def reference(x, norm_g, w_in, b_if, conv_w, conv_b, mlstm_norm_g, w_attn_branch,
              w_mlstm_branch, w_out, final_norm_g):
    bsz, s, _ = x.shape
    f32 = jnp.float32
    for _layer in range(DEPTH):
        hn = _rmsnorm(x, norm_g)
        proj = jnp.einsum('bsd,de->bse', hn, w_in)
        (aq, ak, av, az, mqk, mv, mi, mf, mo, mz, gate_a, gate_m) = jnp.split(
            proj, IN_SPLIT_POINTS, axis=-1)

        heads = lambda t: t.reshape(bsz, s, ATTN_HEADS, ATTN_HEAD_DIM)
        attn = _dilated_attention(heads(aq), heads(ak), heads(av)).reshape(bsz, s, ATTN_WIDTH)
        attn = attn.astype(x.dtype) * jax.nn.silu(az)
        y_a = jnp.einsum('bse,ed->bsd', attn, w_attn_branch)

        qk = _causal_conv_silu(mqk, conv_w, conv_b)
        mq, mk = jnp.split(qk, 2, axis=-1)
        mq = mq.reshape(bsz, s, MLSTM_HEADS, MLSTM_QK_DIM).astype(f32)
        mk = mk.reshape(bsz, s, MLSTM_HEADS, MLSTM_QK_DIM).astype(f32)
        mvh = mv.reshape(bsz, s, MLSTM_HEADS, MLSTM_V_DIM).astype(f32)
        ig = mi.astype(f32) + b_if[:MLSTM_HEADS].astype(f32)
        lf = jax.nn.log_sigmoid(mf.astype(f32) + b_if[MLSTM_HEADS:].astype(f32))
        cell = _mlstm_chunkwise(mq, mk, mvh, ig, lf)
        cell = jax.nn.sigmoid(mo.astype(f32)).reshape(bsz, s, MLSTM_HEADS, MLSTM_V_DIM) * cell
        cell = cell * lax.rsqrt(jnp.mean(cell * cell, axis=-1, keepdims=True) + NORM_EPS)
        cell = cell.reshape(bsz, s, MLSTM_V_WIDTH) * mlstm_norm_g.astype(f32)
        mem = cell.astype(x.dtype) * jax.nn.silu(mz)
        y_m = jnp.einsum('bse,ed->bsd', mem, w_mlstm_branch)

        merged = jax.nn.sigmoid(gate_a) * y_a + jax.nn.sigmoid(gate_m) * y_m
        x = x + jnp.einsum('bsd,de->bse', merged, w_out)
    return _rmsnorm(x, final_norm_g)
```

```python
import contextlib
import numpy as np
import concourse.bass as bass
import concourse.mybir as mybir
from concourse.bass_utils import run_bass_kernel_spmd

F32 = mybir.dt.float32
BF16 = mybir.dt.bfloat16
I32 = mybir.dt.int32
AF = mybir.ActivationFunctionType
ALU = mybir.AluOpType

PE, ACT, DVE, POOL, SP = "pe", "act", "dve", "pool", "sp"
ENGS = (PE, ACT, DVE, POOL, SP)

D = 2048
TOK = 2048
NT = 16
EPS = 1e-6
QSCALE = 128.0 ** -0.5
NEG = -30000.0


class Buf:
    __slots__ = ("name", "writers", "readers", "war", "dsem", "dcount")

    def __init__(self, name):
        self.name = name
        self.writers = []
        self.readers = []
        self.war = []
        self.dsem = None
        self.dcount = 0


class Op:
    __slots__ = ("eng", "fn", "deps", "is_dma", "sig", "semval", "dbuf", "idx")

    def __init__(self, eng, fn, is_dma=False):
        self.eng = eng
        self.fn = fn
        self.deps = []
        self.is_dma = is_dma
        self.sig = False
        self.semval = None
        self.dbuf = None
        self.idx = -1


def _compress(ops):
    last = {}
    for d in ops:
        k = ("dma", id(d.dbuf)) if d.is_dma else d.eng
        if k not in last or last[k].idx < d.idx:
            last[k] = d
    return list(last.values())


class Prog:
    def __init__(self, nc):
        self.nc = nc
        self.ops = {e: [] for e in ENGS}
        self.all_ops = []
        self.final_waits = []
        self.nsem = 0

    def op(self, eng, fn, reads=(), writes=(), joins=(), is_dma=False, dma_buf=None):
        o = Op(eng, fn, is_dma)
        deps = []
        for b in reads:
            deps.extend(b.writers)
        for b in writes:
            deps.extend(b.writers)
            deps.extend(b.readers)
        for b in joins:
            deps.extend(b.readers)
            deps.extend(b.war)
        o.idx = len(self.ops[eng])
        if is_dma:
            o.dbuf = dma_buf
        for b in reads:
            b.readers.append(o)
            if len(b.readers) > 16:
                b.readers = _compress(b.readers)
        for b in writes:
            b.war = _compress(b.writers + b.readers)
            b.writers = [o]
            b.readers = []
        for b in joins:
            b.writers.append(o)
            if len(b.writers) > 16:
                b.writers = _compress(b.writers)
        o.deps = _compress([d for d in deps if d is not o])
        self.ops[eng].append(o)
        self.all_ops.append(o)
        return o

    def emit(self, sem_alloc):
        for o in self.all_ops:
            for d in o.deps:
                if d.is_dma:
                    d.sig = True
                elif d.eng == PE and o.eng == PE and not o.is_dma:
                    continue
                else:
                    d.sig = True
        for o in self.final_waits:
            o.sig = True
        for e in ENGS:
            cnt = 0
            sem = None
            for o in self.ops[e]:
                if o.is_dma:
                    b = o.dbuf
                    if b.dsem is None:
                        b.dsem = sem_alloc("d_" + b.name)
                    b.dcount += 16
                    o.semval = (b.dsem, b.dcount)
                elif o.sig:
                    if sem is None or cnt >= 30000:
                        self.nsem += 1
                        sem = sem_alloc("e_%s%d" % (e, self.nsem))
                        cnt = 0
                    cnt += 1
                    o.semval = (sem, cnt)
        prog = self

        def run(eng_name, eng):
            waited = {}
            for o in prog.ops[eng_name]:
                need = {}
                for d in o.deps:
                    if (not d.is_dma) and d.eng == PE and eng_name == PE and not o.is_dma:
                        continue
                    s, v = d.semval
                    k = id(s)
                    if k not in need or need[k][1] < v:
                        need[k] = (s, v)
                for k, (s, v) in need.items():
                    if waited.get(k, 0) >= v:
                        continue
                    eng.wait_ge(s, v)
                    waited[k] = v
                ins = o.fn(eng)
                if o.is_dma:
                    ins.then_inc(o.semval[0], 16)
                elif o.sig:
                    ins.then_inc(o.semval[0], 1)
            if eng_name == SP:
                need = {}
                for d in prog.final_waits:
                    s, v = d.semval
                    k = id(s)
                    if k not in need or need[k][1] < v:
                        need[k] = (s, v)
                for k, (s, v) in need.items():
                    eng.wait_ge(s, v)

        with self.nc.Block() as block:
            @block.tensor
            def _(e):
                run(PE, e)

            @block.scalar
            def _(e):
                run(ACT, e)

            @block.vector
            def _(e):
                run(DVE, e)

            @block.gpsimd
            def _(e):
                run(POOL, e)

            @block.sync
            def _(e):
                run(SP, e)


O_AQ, O_AK, O_AV, O_AZ = 0, 2048, 4096, 6144
O_MQK, O_MV = 8192, 10240
O_MI, O_MF = 12288, 12296
O_MO, O_MZ = 12304, 14352
O_GA, O_GM = 16400, 18448


def _block_columns():
    blocks = []
    for h in range(16):
        cols = np.concatenate([np.arange(o + 128 * h, o + 128 * h + 128) for o in (O_AQ, O_AK, O_AV, O_AZ)])
        blocks.append(("A%d" % h, "w_in", cols))
    for h in range(8):
        cols = np.concatenate([
            np.arange(O_MQK + 128 * h, O_MQK + 128 * h + 128),
            np.arange(O_MQK + 1024 + 128 * h, O_MQK + 1024 + 128 * h + 128),
            np.arange(O_MV + 256 * h, O_MV + 256 * h + 256)])
        blocks.append(("B1_%d" % h, "w_in", cols))
        cols = np.concatenate([
            np.arange(O_MO + 256 * h, O_MO + 256 * h + 256),
            np.arange(O_MZ + 256 * h, O_MZ + 256 * h + 256)])
        blocks.append(("B2_%d" % h, "w_in", cols))
    blocks.append(("G", "w_in", np.arange(O_MI, O_MI + 16)))
    for i in range(4):
        blocks.append(("GA%d" % i, "w_in", np.arange(O_GA + 512 * i, O_GA + 512 * i + 512)))
    for i in range(4):
        blocks.append(("GM%d" % i, "w_in", np.arange(O_GM + 512 * i, O_GM + 512 * i + 512)))
    for nm, src in (("WA", "w_attn_branch"), ("WM", "w_mlstm_branch"), ("WO", "w_out")):
        for i in range(4):
            blocks.append(("%s%d" % (nm, i), src, np.arange(512 * i, 512 * i + 512)))
    return blocks


_BLOCKS = _block_columns()
_BLK_OFF = {}
_off = 0
for _nm, _src, _cols in _BLOCKS:
    _BLK_OFF[_nm] = (_off, len(_cols))
    _off += 16 * len(_cols)
WST_COLS = _off


def _build_wstream(w_in, w_a, w_m, w_o):
    srcs = {"w_in": w_in, "w_attn_branch": w_a, "w_mlstm_branch": w_m, "w_out": w_o}
    out = np.empty((128, WST_COLS), dtype=np.float32)
    for nm, src, cols in _BLOCKS:
        off, c = _BLK_OFF[nm]
        blk = srcs[src][:, cols]
        blk = blk.reshape(16, 128, c).transpose(1, 0, 2)
        out[:, off:off + 16 * c] = blk.reshape(128, 16 * c)
    return out


def build_nc(dbg=False):
    nc = bass.Bass("TRN2", target_bir_lowering=False)
    x_own = nc.dram_tensor("x_own", [TOK, D], F32, kind="ExternalInput").ap()
    x_pre = nc.dram_tensor("x_pre", [TOK, D], F32, kind="ExternalInput").ap()
    wst = nc.dram_tensor("wst", [128, WST_COLS], F32, kind="ExternalInput").ap()
    g_rep_d = nc.dram_tensor("g_rep", [128, D], F32, kind="ExternalInput").ap()
    fg_rep_d = nc.dram_tensor("fg_rep", [128, D], F32, kind="ExternalInput").ap()
    gm_fm_d = nc.dram_tensor("gm_fm", [128, 16], F32, kind="ExternalInput").ap()
    cw_d = nc.dram_tensor("cw", [128, 64], F32, kind="ExternalInput").ap()
    cb_d = nc.dram_tensor("cb", [128, 16], F32, kind="ExternalInput").ap()
    bif_d = nc.dram_tensor("bif", [128, 256], F32, kind="ExternalInput").ap()
    flag_d = nc.dram_tensor("flag", [128, 2], F32, kind="ExternalInput").ap()
    out_d = nc.dram_tensor("out", [TOK, D], F32, kind="ExternalOutput").ap()
    skind = "ExternalOutput" if dbg else "Internal"
    s_kT = nc.dram_tensor("s_kT", [16, 128, TOK], BF16, kind="Internal").ap()
    s_vT = nc.dram_tensor("s_vT", [16, 128, TOK], BF16, kind="Internal").ap()
    s_attn = nc.dram_tensor("s_attn", [16, 128, TOK], BF16, kind=skind).ap()
    s_mem = nc.dram_tensor("s_mem", [16, 128, TOK], BF16, kind=skind).ap()
    s_ga = nc.dram_tensor("s_ga", [16, 128, TOK], BF16, kind="Internal").ap()
    s_gm = nc.dram_tensor("s_gm", [16, 128, TOK], BF16, kind="Internal").ap()

    with contextlib.ExitStack() as st:
        def sb(name, shape, dt):
            return st.enter_context(nc.sbuf_tensor(name, shape, dt))

        def psum(name, shape, dt):
            return st.enter_context(nc.psum_tensor(name, shape, dt))

        P = Prog(nc)

        R0 = sb("R0", [128, 16, 2048], BF16)
        B_R0 = Buf("R0")
        wsl_t = [sb("wsl%d" % i, [128, 16, 512], BF16) for i in range(2)]
        B_w = [Buf("wsl%d" % i) for i in range(2)]
        ARENA_BYTES = 94 * 1024
        arena = sb("arena", [128, ARENA_BYTES // 2], BF16)
        arena32 = arena.bitcast(F32)
        ident = sb("ident", [128, 128], BF16)
        onesb = sb("onesb", [128, 128], BF16)
        flagones = sb("flagones", [128, 128], BF16)
        cf = sb("cf", [128, 1536], F32)
        ci = sb("ci", [128, 256], I32)
        distPC = cf[:, 0:256]
        maskPC = cf[:, 256:512]
        m01f = cf[:, 512:640]
        onesf = cf[:, 640:768]
        gm_fm = cf[:, 768:784]
        cw = cf[:, 784:848]
        cb = cf[:, 848:864]
        bif = cf[:, 864:1120]
        flag = cf[:, 1120:1122]
        idf = cf[:, 1152:1280]
        B_const = Buf("const")
        cstate = sb("cstate", [128, 8, 257], F32)
        tails = sb("tails", [128, 16, 3], F32)
        B_cstate = [Buf("cstate%d" % h) for h in range(8)]
        B_tails = Buf("tails")
        stat = sb("stat", [128, 64], F32)
        B_stat = Buf("stat")

        banks = [psum("bank%d" % i, [128, 512], F32) for i in range(8)]
        banks_b = [b.bitcast(BF16) for b in banks]
        B_bank = [Buf("bank%d" % i) for i in range(8)]

        def BK(i):
            return [B_bank[i]]

        pj_rr = [0]
        pj_set = [0, 1, 2, 3]

        def next_pj():
            i = pj_set[pj_rr[0] % len(pj_set)]
            pj_rr[0] += 1
            return i

        G = {}

        def grp(name):
            if name not in G:
                G[name] = Buf("g_" + name)
            return G[name]

        class Carver:
            def __init__(self):
                self.off = 0

            def reset(self):
                self.off = 0

            def b16(self, n):
                self.off = (self.off + 3) // 4 * 4
                o = self.off
                self.off += n * 2
                assert self.off <= ARENA_BYTES, self.off
                return arena[:, o // 2:o // 2 + n]

            def f32(self, n):
                self.off = (self.off + 3) // 4 * 4
                o = self.off
                self.off += n * 4
                assert self.off <= ARENA_BYTES, self.off
                return arena32[:, o // 4:o // 4 + n]

        carve = Carver()
        B_arena = Buf("arena_phase")
        B_smem = Buf("s_mem")

        def dma(eng, out, in_, g, R=(), W=(), J=(), **kw):
            return P.op(eng, lambda e, o=out, i=in_, kw=kw: e.dma_start(out=o, in_=i, **kw),
                        reads=R, writes=W, joins=J, is_dma=True, dma_buf=grp(g))

        def mm(out, lhsT, rhs, start, stop, bk, R=()):
            return P.op(PE, lambda e, o=out, l=lhsT, r=rhs, s=start, t=stop:
                        e.matmul(o, lhsT=l, rhs=r, start=s, stop=t), reads=R, writes=BK(bk))

        def tr(out, in_, bk, R=()):
            return P.op(PE, lambda e, o=out, i=in_: e.transpose(out=o, in_=i, identity=ident[:]),
                        reads=list(R) + [B_const], writes=BK(bk))

        def act(out, in_, func, R=(), W=(), J=(), **kw):
            return P.op(ACT, lambda e, o=out, i=in_, f=func, kw=kw: e.activation(out=o, in_=i, func=f, **kw),
                        reads=R, writes=W, joins=J)

        def vop(eng, name, R=(), W=(), J=(), **kw):
            return P.op(eng, lambda e, n=name, kw=kw: getattr(e, n)(**kw), reads=R, writes=W, joins=J)

        def cp(eng, out, in_, R=(), W=(), J=()):
            if eng == ACT:
                return vop(ACT, "copy", R=R, W=W, J=J, out=out, in_=in_)
            return vop(eng, "tensor_copy", R=R, W=W, J=J, out=out, in_=in_)

        B_touch = Buf("touch")

        def touch(bufs, R=()):
            P.op(DVE, lambda e: e.memset(stat[:, 63:64], 0.0), reads=list(R), writes=list(bufs) + [B_touch])

        wload_rr = [0]

        def load_w(name, slot=None):
            off, c = _BLK_OFF[name]
            if slot is None:
                s = wload_rr[0] % 2
                wload_rr[0] += 1
            else:
                s = slot
            src = wst[:, off:off + 16 * c].rearrange("p (k c) -> p k c", c=c)
            dma(POOL, wsl_t[s][:, :, 0:c], src, "wsl%d" % s, W=[B_w[s]], max_dma_last_dim=8192)
            return s

        dma(SP, gm_fm, gm_fm_d, "const", J=[B_const])
        dma(SP, cw, cw_d, "const", J=[B_const])
        dma(SP, cb, cb_d, "const", J=[B_const])
        dma(SP, bif, bif_d, "const", J=[B_const])
        dma(SP, flag, flag_d, "const", J=[B_const])
        B_ci = Buf("ci")
        vop(POOL, "iota", W=[B_ci], out=ci[:, 0:128], pattern=[[1, 128]], base=128, channel_multiplier=-1)
        vop(POOL, "iota", R=[B_ci], W=[B_ci], out=ci[:, 128:256], pattern=[[1, 128]], base=0, channel_multiplier=-1)
        vop(POOL, "tensor_copy", R=[B_ci], W=[B_const], out=distPC, in_=ci[:, :])
        vop(POOL, "memset", R=[B_const], W=[B_const], ap=maskPC, constant=0.0)
        vop(POOL, "memset", R=[B_const], W=[B_const], ap=onesf, constant=1.0)
        vop(POOL, "affine_select", R=[B_const], W=[B_const], out=maskPC[:, 0:128], in_=maskPC[:, 0:128],
            pattern=[[-1, 128]], compare_op=ALU.is_ge, fill=NEG, base=0, channel_multiplier=1)
        vop(POOL, "affine_select", R=[B_const], W=[B_const], out=maskPC[:, 128:256], in_=maskPC[:, 128:256],
            pattern=[[1, 128]], compare_op=ALU.is_ge, fill=NEG, base=0, channel_multiplier=-1)
        vop(POOL, "affine_select", R=[B_const], W=[B_const], out=m01f, in_=onesf,
            pattern=[[1, 128]], compare_op=ALU.is_ge, fill=0.0, base=0, channel_multiplier=-1)
        vop(POOL, "affine_select", R=[B_const], W=[B_const], out=idf, in_=onesf,
            pattern=[[-1, 128]], compare_op=ALU.is_equal, fill=0.0, base=0, channel_multiplier=1)
        vop(POOL, "tensor_copy", R=[B_const], W=[B_const], out=ident[:], in_=idf)
        vop(POOL, "tensor_copy", R=[B_const], W=[B_const], out=onesb[:], in_=onesf)
        vop(DVE, "tensor_scalar", R=[B_const], W=[B_const], out=flagones[:], in0=onesf,
            scalar1=flag[:, 0:1], scalar2=None, op0=ALU.mult)
        vop(DVE, "memset", W=[B_tails], ap=tails[:], constant=0.0)
        for h in range(8):
            vop(DVE, "memset", W=[B_cstate[h]], ap=cstate[:, h, :], constant=0.0)
        vop(DVE, "memset", R=[B_const], W=[B_const], ap=stat[:, 62:63], constant=float(np.log(QSCALE)))
        CONSTS = [B_const]

        def phase_norm(x_src):
            carve.reset()
            xt = [carve.f32(2048) for _ in range(4)]
            g_rep = carve.f32(2048)
            xn = [carve.b16(2048) for _ in range(2)]
            junk = carve.b16(2048)
            B_xt = [Buf("xt%d" % i) for i in range(4)]
            B_xn = [Buf("xn0"), Buf("xn1")]
            B_g = Buf("g_rep")
            B_junk = Buf("junk")
            users = B_xt + B_xn + [B_g, B_junk]
            touch(users, R=[B_arena])
            dma(SP, g_rep, g_rep_d, "g_rep", W=[B_g])
            first_r0 = [True]
            B_st = [Buf("nst%d" % i) for i in range(4)]

            def stage1(j):
                s = j % 4
                s2_ = j % 2
                bs_ = B_st[j % 4]
                dma(SP, xt[s], x_src[j * 128:(j + 1) * 128, :], "xt%d" % s, W=[B_xt[s]])
                act(junk, xt[s], AF.Square, R=[B_xt[s]], W=[B_junk, bs_], accum_out=stat[:, j:j + 1])
                act(stat[:, 16 + j:17 + j], stat[:, j:j + 1], AF.Sqrt, R=[bs_], W=[bs_], scale=1.0 / D, bias=EPS)
                vop(DVE, "reciprocal", R=[bs_], W=[bs_], out=stat[:, 32 + j:33 + j], in_=stat[:, 16 + j:17 + j])
                vop(DVE, "scalar_tensor_tensor", R=[bs_, B_xt[s], B_g], W=[B_xn[s2_]], out=xn[s2_], in0=xt[s],
                    scalar=stat[:, 32 + j:33 + j], in1=g_rep, op0=ALU.mult, op1=ALU.mult)

            def stage2(j):
                s2_ = j % 2
                for q in range(4):
                    bk = next_pj()
                    for i in range(4):
                        kk = 4 * q + i
                        tr(banks_b[bk][:, i * 128:(i + 1) * 128], xn[s2_][:, kk * 128:(kk + 1) * 128], bk, R=[B_xn[s2_]])
                    src = banks_b[bk][:, 0:512].rearrange("p (a b) -> p a b", b=128)
                    dst = R0[:, 4 * q:4 * q + 4, j * 128:(j + 1) * 128]
                    cp(ACT if q % 2 == 0 else DVE, dst, src, W=BK(bk) + ([B_R0] if first_r0[0] else []), J=[] if first_r0[0] else [B_R0])
                    first_r0[0] = False

            stage1(0)
            for j in range(NT):
                if j + 1 < NT:
                    stage1(j + 1)
                stage2(j)
            return users

        def fence(bufs):
            P.op(DVE, lambda e: e.memset(stat[:, 63:64], 0.0), reads=list(bufs), writes=[B_arena, B_touch] + list(bufs))

        def proj_fm(s, c0, evac):
            for tt in range(4):
                bk = next_pj()
                for k in range(16):
                    mm(banks[bk][:, :], wsl_t[s][:, k, c0:c0 + 128], R0[:, k, tt * 512:(tt + 1) * 512],
                       k == 0, k == 15, bk, R=[B_w[s], B_R0])
                evac(tt, bk)

        def proj_tm(s, c0, ncols, j, bk, bc0):
            for k in range(16):
                mm(banks[bk][:, bc0:bc0 + ncols], R0[:, k, j * 128:(j + 1) * 128], wsl_t[s][:, k, c0:c0 + ncols],
                   k == 0, k == 15, bk, R=[B_w[s], B_R0])

        def gate_math(gs, B_gs):
            s = load_w("G")
            bk = next_pj()
            for j in range(NT):
                proj_tm(s, 0, 16, j, bk, j * 16)
            vop(DVE, "tensor_tensor", R=CONSTS, W=BK(bk) + [B_gs], out=gs["gpre"], in0=banks[bk][:, 0:256], in1=bif, op=ALU.add)
            gp3 = gs["gpre"].rearrange("p (j c) -> p j c", c=16)
            ig3 = gp3[:, :, 0:8]
            fp3 = gp3[:, :, 8:16]
            lf3 = gs["lf"].rearrange("p (j c) -> p j c", c=8)
            act(lf3, fp3, AF.Exp, R=[B_gs], W=[B_gs], scale=-1.0)
            act(gs["lf"], gs["lf"], AF.Ln, R=[B_gs], W=[B_gs], bias=1.0)
            vop(DVE, "tensor_scalar", R=[B_gs], W=[B_gs], out=gs["lf"], in0=gs["lf"], scalar1=-1.0, scalar2=None, op0=ALU.mult)
            bk2 = next_pj()
            mm(banks[bk2][:, 0:128], m01f, gs["lf"], True, True, bk2, R=[B_gs] + CONSTS)
            mm(banks[bk2][:, 128:256], onesf, gs["lf"], True, True, bk2, R=[B_gs] + CONSTS)
            d3 = gs["d"].rearrange("p (j c) -> p j c", c=8)
            vop(DVE, "tensor_tensor", R=[B_gs], W=BK(bk2) + [B_gs], out=d3, in0=ig3,
                in1=banks[bk2][:, 0:128].rearrange("p (j c) -> p j c", c=8), op=ALU.subtract)
            act(gs["u"], gs["d"], AF.Exp, R=[B_gs] + CONSTS, W=[B_gs], bias=stat[:, 62:63])
            vop(DVE, "tensor_tensor", R=[B_gs], W=BK(bk2) + [B_gs], out=gs["d2"], in0=gs["d"], in1=banks[bk2][:, 128:256], op=ALU.add)
            act(gs["w"], gs["d2"], AF.Exp, R=[B_gs] + CONSTS, W=[B_gs], bias=stat[:, 62:63])
            act(gs["ec"], banks[bk2][:, 128:256], AF.Exp, R=[B_gs], W=BK(bk2) + [B_gs])
            act(gs["emb"], banks[bk2][:, 0:128], AF.Exp, R=[B_gs], W=BK(bk2) + [B_gs], scale=-1.0)

        def carve_gates():
            gs = {"gpre": carve.f32(256)}
            for n in ("lf", "d", "d2", "u", "w", "ec", "emb"):
                gs[n] = carve.f32(128)
            return gs

        def conv_silu(s, c0, coltile, tail_idx, upre, ybuf, outT, B_u, B_y, B_out, save_tail):
            vop(DVE, "tensor_copy", R=[B_tails], W=[B_u], out=upre[:, 0:3], in_=tails[:, tail_idx, :])

            def ev(tt, bk):
                cp(ACT, upre[:, 3 + tt * 512:3 + (tt + 1) * 512], banks[bk][:, :], W=BK(bk), J=[B_u])
            proj_fm(s, c0, ev)
            if save_tail:
                vop(DVE, "tensor_copy", R=[B_u], W=[B_tails], out=tails[:, tail_idx, :], in_=upre[:, 2048:2051])
            vop(DVE, "tensor_scalar", R=[B_u] + CONSTS, W=[B_y], out=ybuf, in0=upre[:, 0:2048],
                scalar1=cw[:, coltile * 4:coltile * 4 + 1], scalar2=cb[:, coltile:coltile + 1], op0=ALU.mult, op1=ALU.add)
            for tp in range(1, 4):
                vop(DVE, "scalar_tensor_tensor", R=[B_u] + CONSTS, W=[B_y], out=ybuf, in0=upre[:, tp:tp + 2048],
                    scalar=cw[:, coltile * 4 + tp:coltile * 4 + tp + 1], in1=ybuf, op0=ALU.mult, op1=ALU.add)
            act(outT, ybuf, AF.Silu, R=[B_y], W=[B_out])

        def interleave(*gens):
            gens = list(gens)
            while gens:
                for g in list(gens):
                    try:
                        next(g)
                    except StopIteration:
                        gens.remove(g)

        HT = 1024
        HJ = 8

        def carve_mlstm(prefix):
            M = {"prefix": prefix}
            M["gs"] = carve_gates()
            M["B_gs"] = Buf("gs")
            M["upre"] = carve.f32(HT + 4)
            M["ybuf"] = carve.f32(HT)
            M["B_u"], M["B_y"] = Buf("upre"), Buf("ybuf")
            M["vpp"] = [carve.b16(258) for _ in range(2)]
            M["B_vpp"] = [Buf("vpp0"), Buf("vpp1")]
            sets = []
            for i_ in range(2):
                d_ = dict(i=i_, kT=carve.b16(HT), ktok=carve.b16(HJ * 128), vaug=carve.b16(HJ * 257 + 1),
                          B_kT=Buf("kTm%d" % i_), B_ktok=Buf("ktok%d" % i_), B_vaug=Buf("vaug%d" % i_))
                if not prefix:
                    d_.update(qT=carve.b16(HT), sigo=carve.b16(HJ * 256), zT=carve.b16(2 * HT),
                              B_qT=Buf("qTm%d" % i_), B_sigo=Buf("sigo%d" % i_), B_zT=Buf("zTm%d" % i_))
                sets.append(d_)
            M["sets"] = sets
            allb = [M["B_gs"], M["B_u"], M["B_y"]] + M["B_vpp"]
            for d_ in sets:
                allb += [v for k_, v in d_.items() if k_.startswith("B_")]
            if not prefix:
                M["cellf"] = [carve.f32(256) for _ in range(2)]
                M["Cb"] = carve.b16((HJ + 1) * 258)
                M["scT"] = [carve.b16(128) for _ in range(2)]
                M["vp1"] = [carve.b16(258) for _ in range(4)]
                M["celln"] = [carve.b16(256) for _ in range(2)]
                M["mem_stg"] = carve.b16(2 * HT)
                M["junk256"] = carve.b16(256)
                M["B_cell"] = [Buf("cell0"), Buf("cell1")]
                M["B_Cb"] = [Buf("Cb0"), Buf("Cb1")]
                M["B_sc"] = [Buf("sc0"), Buf("sc1")]
                M["B_vp1"] = [Buf("vp1%d" % i) for i in range(4)]
                M["B_celln"] = [Buf("celln0"), Buf("celln1")]
                M["B_mstg"] = Buf("mem_stg")
                M["B_j256"] = Buf("junk256")
                allb += M["B_cell"] + M["B_Cb"] + M["B_sc"] + M["B_vp1"] + M["B_celln"] + [M["B_mstg"], M["B_j256"]]
            M["all"] = allb
            touch(allb, R=[B_arena])
            return M

        slotsM = {}

        def convM(M, s, c0, coltile, tail_idx, t0, outT, B_out):
            upre, ybuf, B_u, B_y = M["upre"], M["ybuf"], M["B_u"], M["B_y"]
            vop(DVE, "tensor_copy", R=[B_tails], W=[B_u], out=upre[:, 0:3], in_=tails[:, tail_idx, :])
            for tt in range(HT // 512):
                bk = next_pj()
                for k in range(16):
                    mm(banks[bk][:, :], wsl_t[s][:, k, c0:c0 + 128], R0[:, k, t0 + tt * 512:t0 + (tt + 1) * 512],
                       k == 0, k == 15, bk, R=[B_w[s], B_R0])
                yield
                cp(ACT, upre[:, 3 + tt * 512:3 + (tt + 1) * 512], banks[bk][:, :], W=BK(bk), J=[B_u])
            vop(DVE, "tensor_copy", R=[B_u], W=[B_tails], out=tails[:, tail_idx, :], in_=upre[:, HT:HT + 3])
            vop(DVE, "tensor_scalar", R=[B_u] + CONSTS, W=[B_y], out=ybuf, in0=upre[:, 0:HT],
                scalar1=cw[:, coltile * 4:coltile * 4 + 1], scalar2=cb[:, coltile:coltile + 1], op0=ALU.mult, op1=ALU.add)
            for tp in range(1, 4):
                vop(DVE, "scalar_tensor_tensor", R=[B_u] + CONSTS, W=[B_y], out=ybuf, in0=upre[:, tp:tp + HT],
                    scalar=cw[:, coltile * 4 + tp:coltile * 4 + tp + 1], in1=ybuf, op0=ALU.mult, op1=ALU.add)
            act(outT, ybuf, AF.Silu, R=[B_y], W=[B_out])
            yield

        def projM(M, h, hf, bs):
            prefix = M["prefix"]
            t0 = hf * HT
            if hf == 0 and h == 0:
                if prefix:
                    slotsM[0] = (load_w("B1_0", slot=0), None)
                else:
                    slotsM[0] = (load_w("B1_0", slot=0), load_w("B2_0", slot=1))
            if prefix and hf == 0 and h + 1 < 8:
                slotsM[h + 1] = (load_w("B1_%d" % (h + 1), slot=(h + 1) % 2), None)
            s, s2 = slotsM[h]
            ktok3 = bs["ktok"].rearrange("p (j c) -> p j c", c=128)
            vaug3 = bs["vaug"][:, 0:HJ * 257].rearrange("p (j c) -> p j c", c=257)
            yield
            yield from convM(M, s, 128, 8 + h, 8 + h, t0, bs["kT"], bs["B_kT"])
            if not prefix:
                yield from convM(M, s, 0, h, h, t0, bs["qT"], bs["B_qT"])
            elif hf == 1:
                bk = next_pj()
                for k in range(16):
                    mm(banks[bk][:, 0:128], wsl_t[s][:, k, 0:128], R0[:, k, 1920:2048], k == 0, k == 15, bk, R=[B_w[s], B_R0])
                vop(DVE, "tensor_copy", W=BK(bk) + [B_tails], out=tails[:, h, :], in_=banks[bk][:, 125:128])
                yield
            ones_src = flag[:, 0:1] if prefix else onesf[:, 0:1]
            for j2 in range(HJ // 2):
                bk = next_pj()
                for i in range(2):
                    jt = hf * HJ + 2 * j2 + i
                    for k in range(16):
                        mm(banks[bk][:, i * 256:(i + 1) * 256], R0[:, k, jt * 128:(jt + 1) * 128], wsl_t[s][:, k, 256:512],
                           k == 0, k == 15, bk, R=[B_w[s], B_R0])
                    yield
                cp(ACT, vaug3[:, 2 * j2:2 * j2 + 2, 0:256], banks[bk][:, 0:512].rearrange("p (a b) -> p a b", b=256),
                   W=BK(bk) + ([bs["B_vaug"]] if j2 == 0 else []), J=[] if j2 == 0 else [bs["B_vaug"]])
            for jj in range(HJ):
                vop(DVE, "tensor_copy", R=CONSTS, J=[bs["B_vaug"]], out=vaug3[:, jj, 256:257], in_=ones_src)
            if (not prefix) and hf == 1 and h + 1 < 8:
                slotsM[h + 1] = (load_w("B1_%d" % (h + 1), slot=0), None)
            for q in range(HJ // 4):
                bk = next_pj()
                for i in range(4):
                    jj = 4 * q + i
                    tr(banks_b[bk][:, i * 128:(i + 1) * 128], bs["kT"][:, jj * 128:(jj + 1) * 128], bk, R=[bs["B_kT"]])
                cp(ACT, ktok3[:, 4 * q:4 * q + 4, :], banks_b[bk][:, 0:512].rearrange("p (a b) -> p a b", b=128),
                   W=BK(bk) + ([bs["B_ktok"]] if q == 0 else []), J=[] if q == 0 else [bs["B_ktok"]])
                yield
            if prefix:
                return
            sigo3 = bs["sigo"].rearrange("p (j c) -> p j c", c=256)
            zT3 = bs["zT"].rearrange("p (a t) -> p a t", a=2)
            for j2 in range(HJ // 2):
                bk = next_pj()
                for i in range(2):
                    jt = hf * HJ + 2 * j2 + i
                    for k in range(16):
                        mm(banks[bk][:, i * 256:(i + 1) * 256], R0[:, k, jt * 128:(jt + 1) * 128], wsl_t[s2][:, k, 0:256],
                           k == 0, k == 15, bk, R=[B_w[s2], B_R0])
                    yield
                cp(ACT, sigo3[:, 2 * j2:2 * j2 + 2, :], banks[bk][:, 0:512].rearrange("p (a b) -> p a b", b=256),
                   W=BK(bk) + ([bs["B_sigo"]] if j2 == 0 else []), J=[] if j2 == 0 else [bs["B_sigo"]])
            for i in range(2):
                for tt in range(HT // 512):
                    bk = next_pj()
                    for k in range(16):
                        mm(banks[bk][:, :], wsl_t[s2][:, k, 256 + 128 * i:384 + 128 * i], R0[:, k, t0 + tt * 512:t0 + (tt + 1) * 512],
                           k == 0, k == 15, bk, R=[B_w[s2], B_R0])
                    yield
                    first = (i == 0 and tt == 0)
                    cp(ACT, zT3[:, i, tt * 512:(tt + 1) * 512], banks[bk][:, :],
                       W=BK(bk) + ([bs["B_zT"]] if first else []), J=[] if first else [bs["B_zT"]])
            if hf == 1 and h + 1 < 8:
                slotsM[h + 1] = (slotsM[h + 1][0], load_w("B2_%d" % (h + 1), slot=1))
            act(bs["zT"], bs["zT"], AF.Silu, R=[bs["B_zT"]], W=[bs["B_zT"]])
            act(bs["sigo"], bs["sigo"], AF.Sigmoid, R=[bs["B_sigo"]], W=[bs["B_sigo"]])
            yield

        def recM(M, h, hf, bs):
            prefix = M["prefix"]
            gs, B_gs = M["gs"], M["B_gs"]
            vpp, B_vpp = M["vpp"], M["B_vpp"]
            ktok3 = bs["ktok"].rearrange("p (j c) -> p j c", c=128)
            vaug3 = bs["vaug"][:, 0:HJ * 257].rearrange("p (j c) -> p j c", c=257)
            t0 = hf * HT
            if not prefix:
                Cb3 = M["Cb"].rearrange("p (j c) -> p j c", c=258)
                sigo3 = bs["sigo"].rearrange("p (j c) -> p j c", c=256)
                zT3 = bs["zT"].rearrange("p (a t) -> p a t", a=2)
                mem3 = M["mem_stg"].rearrange("p (a t) -> p a t", a=2)
                cp(ACT, Cb3[:, 0, 0:257], cstate[:, h, :], R=[B_cstate[h]], W=[M["B_Cb"][0]])

            TBs = (2, 3)
            B_sst = M.setdefault("B_sst", [Buf("sst0"), Buf("sst1")])

            def need_step(jj):
                return prefix or not (hf == 1 and jj == HJ - 1)

            def st0(jj):
                j = hf * HJ + jj
                sl = jj % 2
                col = j * 8 + h
                if need_step(jj):
                    vop(DVE, "tensor_scalar", R=[bs["B_vaug"], B_gs], W=[B_vpp[sl]], out=vpp[sl][:, 0:257], in0=vaug3[:, jj, :],
                        scalar1=gs["w"][:, col:col + 1], scalar2=None, op0=ALU.mult)
                if prefix:
                    return
                obk = 6 + sl
                s4 = jj % 4
                mm(banks[obk][:, 384:512], bs["kT"][:, jj * 128:(jj + 1) * 128], bs["qT"][:, jj * 128:(jj + 1) * 128], True, True, obk,
                   R=[bs["B_kT"], bs["B_qT"]])
                vop(DVE, "tensor_scalar", R=[bs["B_vaug"], B_gs], W=[M["B_vp1"][s4]], out=M["vp1"][s4][:, 0:257], in0=vaug3[:, jj, :],
                    scalar1=gs["u"][:, col:col + 1], scalar2=None, op0=ALU.mult)

            def st1(jj):
                sl = jj % 2
                if need_step(jj):
                    bk = 4 + sl
                    mm(banks[bk][:, 0:257], ktok3[:, jj, :], vpp[sl][:, 0:257], True, True, bk, R=[bs["B_ktok"], B_vpp[sl]])
                if prefix:
                    return
                obk = 6 + sl
                vop(DVE, "tensor_tensor", R=CONSTS, W=BK(obk) + [M["B_sc"][sl]], out=M["scT"][sl], in0=banks[obk][:, 384:512],
                    in1=m01f, op=ALU.mult)

            def st2(jj):
                j = hf * HJ + jj
                sl = jj % 2
                col = j * 8 + h
                if need_step(jj):
                    bk = 4 + sl
                    vop(DVE, "scalar_tensor_tensor", R=[B_gs], W=BK(bk) + [B_cstate[h]], out=cstate[:, h, :],
                        in0=cstate[:, h, :], scalar=gs["ec"][:, col:col + 1], in1=banks[bk][:, 0:257], op0=ALU.mult, op1=ALU.add)
                    if not prefix:
                        cp(DVE, Cb3[:, jj + 1, 0:257], cstate[:, h, :], R=[B_cstate[h]], J=[M["B_Cb"][(jj + 1) % 2]])
                if prefix:
                    return
                obk = 6 + sl
                s4 = jj % 4
                Ops_ = banks[obk][:, 0:257]
                mm(Ops_, M["scT"][sl], M["vp1"][s4][:, 0:257], True, False, obk, R=[M["B_sc"][sl], M["B_vp1"][s4]])
                mm(Ops_, bs["qT"][:, jj * 128:(jj + 1) * 128], Cb3[:, jj, 0:257], False, True, obk, R=[bs["B_qT"], M["B_Cb"][jj % 2]])

            def st3(jj):
                j = hf * HJ + jj
                sl = jj % 2
                col = j * 8 + h
                obk = 6 + sl
                sc_ = stat[:, 48 + 4 * sl:52 + 4 * sl]
                vop(DVE, "tensor_scalar", W=BK(obk) + [B_sst[sl]], out=sc_[:, 0:1], in0=banks[obk][:, 256:257],
                    scalar1=-1.0, scalar2=None, op0=ALU.mult)
                vop(DVE, "tensor_scalar", R=[B_gs], W=BK(obk) + [B_sst[sl]], out=sc_[:, 1:2], in0=banks[obk][:, 256:257],
                    scalar1=sc_[:, 0:1], scalar2=gs["emb"][:, col:col + 1], op0=ALU.max, op1=ALU.max)
                vop(DVE, "reciprocal", R=[B_sst[sl]], W=[B_sst[sl]], out=sc_[:, 0:1], in_=sc_[:, 1:2])
                vop(DVE, "scalar_tensor_tensor", R=[B_sst[sl], bs["B_sigo"]], W=BK(obk) + [M["B_cell"][sl]], out=M["cellf"][sl],
                    in0=banks[obk][:, 0:256], scalar=sc_[:, 0:1], in1=sigo3[:, jj, :], op0=ALU.mult, op1=ALU.mult)

            def st4(jj):
                sl = jj % 2
                sc_ = stat[:, 48 + 4 * sl:52 + 4 * sl]
                act(M["junk256"], M["cellf"][sl], AF.Square, R=[M["B_cell"][sl]], W=[M["B_j256"], B_sst[sl]], accum_out=sc_[:, 2:3])
                act(sc_[:, 2:3], sc_[:, 2:3], AF.Sqrt, R=[B_sst[sl]], W=[B_sst[sl]], scale=1.0 / 256.0, bias=EPS)

            def st5(jj):
                sl = jj % 2
                sc_ = stat[:, 48 + 4 * sl:52 + 4 * sl]
                vop(DVE, "reciprocal", R=[B_sst[sl]], W=[B_sst[sl]], out=sc_[:, 3:4], in_=sc_[:, 2:3])
                vop(DVE, "tensor_scalar", R=[B_sst[sl], M["B_cell"][sl]], W=[M["B_celln"][sl]], out=M["celln"][sl], in0=M["cellf"][sl],
                    scalar1=sc_[:, 3:4], scalar2=None, op0=ALU.mult)

            def st6(jj):
                sl = jj % 2
                TB = TBs[sl]
                for i in range(2):
                    tr(banks_b[TB][:, i * 128:(i + 1) * 128], M["celln"][sl][:, i * 128:(i + 1) * 128], TB, R=[M["B_celln"][sl]])

            def st7(jj):
                sl = jj % 2
                TB = TBs[sl]
                for i in range(2):
                    first = (jj == 0 and i == 0)
                    vop(DVE, "scalar_tensor_tensor", R=[bs["B_zT"]] + CONSTS,
                        W=BK(TB) + ([M["B_mstg"]] if first else []), J=[] if first else [M["B_mstg"]],
                        out=mem3[:, i, jj * 128:(jj + 1) * 128], in0=banks_b[TB][:, i * 128:(i + 1) * 128],
                        scalar=gm_fm[:, 2 * h + i:2 * h + i + 1], in1=zT3[:, i, jj * 128:(jj + 1) * 128], op0=ALU.mult, op1=ALU.mult)

            stages = [st0, st1, st2] if prefix else [st0, st1, st2, st3, st4, st5, st6, st7]
            ns = len(stages)
            yield
            for it in range(HJ + ns - 1):
                for k in range(ns - 1, -1, -1):
                    jj = it - k
                    if 0 <= jj < HJ:
                        stages[k](jj)
                yield
            if not prefix:
                for i in range(2):
                    dma(SP, s_mem[2 * h + i][:, t0:t0 + HT], mem3[:, i, :], "mstg", R=[M["B_mstg"]], J=[B_smem])
                yield

        def run_units(M, nheads):
            units = [(h, hf) for h in range(nheads) for hf in range(2)]
            pj_set[:] = [0, 1]
            interleave(projM(M, units[0][0], units[0][1], M["sets"][0]))
            for ui, (h, hf) in enumerate(units):
                grec = recM(M, h, hf, M["sets"][ui % 2])
                gproj = None
                if ui + 1 < len(units):
                    h2, hf2 = units[ui + 1]
                    gproj = projM(M, h2, hf2, M["sets"][(ui + 1) % 2])
                while grec is not None or gproj is not None:
                    if grec is not None:
                        try:
                            next(grec)
                        except StopIteration:
                            grec = None
                    for _ in range(2):
                        if gproj is not None:
                            try:
                                next(gproj)
                            except StopIteration:
                                gproj = None
            pj_set[:] = [0, 1, 2, 3]

        arena_users = phase_norm(x_pre)
        fence(arena_users)
        carve.reset()
        stg = [carve.b16(2048) for _ in range(4)]
        B_stg = [Buf("stg%d" % i) for i in range(4)]
        B_skv = Buf("s_kv")
        touch(B_stg, R=[B_arena])
        MP = carve_mlstm(True)
        gate_math(MP["gs"], MP["B_gs"])
        for h in range(16):
            s = load_w("A%d" % h)
            for which, (c0, dst) in enumerate(((128, s_kT), (256, s_vT))):
                si = (2 * h + which) % 4

                def ev(tt, bk, si=si):
                    cp(ACT if tt % 2 == 0 else DVE, stg[si][:, tt * 512:(tt + 1) * 512], banks[bk][:, :],
                       W=BK(bk) + ([B_stg[si]] if tt == 0 else []), J=[] if tt == 0 else [B_stg[si]])
                proj_fm(s, c0, ev)
                dma(SP, dst[h], stg[si], "stg%d" % si, R=[B_stg[si]], J=[B_skv])
        run_units(MP, 8)
        fence(B_stg + MP["all"])


        arena_users = phase_norm(x_own)
        fence(arena_users)

        carve.reset()
        setsA = []
        for i_ in range(2):
            setsA.append(dict(qT=carve.b16(2048), kT=carve.b16(4096), vT=carve.b16(4096), zT=carve.b16(2048),
                              B_qT=Buf("qT%d" % i_), B_kT=Buf("kT%d" % i_), B_vT=Buf("vT%d" % i_), B_zT=Buf("zT%d" % i_), i=i_))
        NVT = 69
        Vt = carve.b16(NVT * 128)
        Pt = [carve.b16(256) for _ in range(2)]
        attn_stg = carve.b16(2048)
        biasT = [carve.f32(256) for _ in range(3)]
        tmpS = [carve.f32(256) for _ in range(2)]
        acc = carve.f32(4096)
        acc3 = acc.rearrange("p (a t) -> p a t", a=2)
        Otmp = [carve.f32(256) for _ in range(2)]
        B_ot = [Buf("Otmp0"), Buf("Otmp1")]
        B_Vt, B_astg, B_bias, B_acc = Buf("Vt"), Buf("attn_stg"), Buf("biasT"), Buf("acc")
        B_fin = Buf("acc_fin")
        B_Pt = [Buf("Pt%d" % i) for i in range(2)]
        B_tmpS = [Buf("tmpS%d" % i) for i in range(2)]
        allA = [B_Vt, B_astg, B_bias, B_acc, B_fin] + B_Pt + B_tmpS + B_ot
        for st_ in setsA:
            allA += [st_["B_qT"], st_["B_kT"], st_["B_vT"], st_["B_zT"]]
        touch(allA, R=[B_arena])
        B_sattn = Buf("s_attn")

        vt_base = {1: 0, 4: 17, 16: 37}

        def vt_idx(r, blk, ph):
            return vt_base[r] + blk * r + ph

        def projA(h, bs):
            qT, kT_all, vT_all, zT = bs["qT"], bs["kT"], bs["vT"], bs["zT"]
            B_qT, B_kT, B_vT, B_zT = bs["B_qT"], bs["B_kT"], bs["B_vT"], bs["B_zT"]
            s = slotA[h]
            dma(SP, kT_all[:, 0:2048], s_kT[h], "kTl%d" % bs["i"], R=[B_skv], W=[B_kT])
            dma(SP, vT_all[:, 0:2048], s_vT[h], "vTl%d" % bs["i"], R=[B_skv], W=[B_vT])
            yield

            def ev_q(tt, bk):
                act(qT[:, tt * 512:(tt + 1) * 512], banks[bk][:, :], AF.Copy,
                    W=BK(bk) + ([B_qT] if tt == 0 else []), J=[] if tt == 0 else [B_qT], scale=QSCALE)

            def ev_k(tt, bk):
                cp(ACT, kT_all[:, 2048 + tt * 512:2048 + (tt + 1) * 512], banks[bk][:, :], W=BK(bk), J=[B_kT])

            def ev_v(tt, bk):
                cp(ACT, vT_all[:, 2048 + tt * 512:2048 + (tt + 1) * 512], banks[bk][:, :], W=BK(bk), J=[B_vT])

            def ev_z(tt, bk):
                cp(ACT, zT[:, tt * 512:(tt + 1) * 512], banks[bk][:, :],
                   W=BK(bk) + ([B_zT] if tt == 0 else []), J=[] if tt == 0 else [B_zT])
            for c0, ev in ((256, ev_v), (128, ev_k), (0, ev_q), (384, ev_z)):
                for tt in range(4):
                    bk = next_pj()
                    for k in range(16):
                        mm(banks[bk][:, :], wsl_t[s][:, k, c0:c0 + 128], R0[:, k, tt * 512:(tt + 1) * 512],
                           k == 0, k == 15, bk, R=[B_w[s], B_R0])
                        if k % 2 == 1:
                            yield
                    ev(tt, bk)

        def attnA(h, bs):
            qT, kT_all, vT_all, zT = bs["qT"], bs["kT"], bs["vT"], bs["zT"]
            B_qT, B_kT, B_vT, B_zT = bs["B_qT"], bs["B_kT"], bs["B_vT"], bs["B_zT"]
            tiles = []
            for r in (1, 4, 16):
                span = 128 * r
                nsp = 2048 // span
                for blk in range(nsp + 1):
                    for ph in range(r):
                        tiles.append((vt_idx(r, blk, ph), 2048 + (blk - 1) * span + ph, r))
            TB = 3
            act(zT, zT, AF.Silu, R=[B_zT], W=[B_zT])
            for g0 in range(0, len(tiles), 4):
                grp_ = tiles[g0:g0 + 4]
                for i, (vi, start, r) in enumerate(grp_):
                    tr(banks_b[TB][:, i * 128:(i + 1) * 128], vT_all[:, start:start + 127 * r + 1:r], TB, R=[B_vT])
                vi0 = grp_[0][0]
                n = len(grp_)
                assert all(grp_[i][0] == vi0 + i for i in range(n))
                cp(ACT, Vt[:, vi0 * 128:(vi0 + n) * 128], banks_b[TB][:, 0:n * 128],
                   W=BK(TB) + ([B_Vt] if g0 == 0 else []), J=[] if g0 == 0 else [B_Vt])
                yield (2 if (g0 // 4) % 2 == 0 else 1)
            slope = 2.0 ** (-(h + 1) / 2.0)
            for pi, r in enumerate((1, 4, 16)):
                vop(DVE, "scalar_tensor_tensor", R=CONSTS, W=[B_bias] if pi == 0 else [], J=[] if pi == 0 else [B_bias],
                    out=biasT[pi], in0=distPC, scalar=float(-slope * r), in1=maskPC, op0=ALU.mult, op1=ALU.add)
            tl = []
            for pi, r in enumerate((1, 4, 16)):
                span = 128 * r
                for a in range(2048 // span):
                    for ph in range(r):
                        tl.append((pi, r, a, ph))
            state = {"first_acc": True}

            def emit_S(ti):
                pi, r, a, ph = tl[ti]
                span = 128 * r
                qs = a * span + ph
                pcur = 2048 + qs
                pprev = pcur - span
                sl = ti % 2
                sbk = 4 + sl
                Sps = banks[sbk][:, 0:256]
                qsl = qT[:, qs:qs + 127 * r + 1:r]
                mm(Sps[:, 0:128], kT_all[:, pprev:pprev + 127 * r + 1:r], qsl, True, True, sbk, R=[B_kT, B_qT])
                mm(Sps[:, 128:256], kT_all[:, pcur:pcur + 127 * r + 1:r], qsl, True, True, sbk, R=[B_kT, B_qT])
                vop(DVE, "tensor_tensor", R=[B_bias], W=BK(sbk) + [B_tmpS[sl]], out=tmpS[sl], in0=Sps, in1=biasT[pi], op=ALU.add)
                act(Pt[sl], tmpS[sl], AF.Exp, R=[B_tmpS[sl]], W=[B_Pt[sl]])

            def emit_PV(ti):
                pi, r, a, ph = tl[ti]
                span = 128 * r
                qs = a * span + ph
                sl = ti % 2
                obk = 6 + sl
                Ops_ = banks[obk][:, 0:256]
                vprev = vt_idx(r, a, ph)
                vcur = vt_idx(r, a + 1, ph)
                mm(Ops_[:, 0:128], Vt[:, vprev * 128:(vprev + 1) * 128], Pt[sl][:, 0:128], True, False, obk, R=[B_Vt, B_Pt[sl]])
                mm(Ops_[:, 0:128], Vt[:, vcur * 128:(vcur + 1) * 128], Pt[sl][:, 128:256], False, True, obk, R=[B_Vt, B_Pt[sl]])
                ones_prev = flagones if a == 0 else onesb
                mm(Ops_[:, 128:256], ones_prev[:], Pt[sl][:, 0:128], True, False, obk, R=[B_Pt[sl]] + CONSTS)
                mm(Ops_[:, 128:256], onesb[:], Pt[sl][:, 128:256], False, True, obk, R=[B_Pt[sl]] + CONSTS)
                dst = acc3[:, :, qs:qs + 127 * r + 1:r]
                src = Ops_.rearrange("p (a t) -> p a t", a=2)
                if pi == 0:
                    fa = state["first_acc"]
                    cp(ACT, dst, src, W=BK(obk) + ([B_acc] if fa else []), J=[] if fa else [B_acc])
                    state["first_acc"] = False
                else:
                    cp(ACT, Otmp[sl], Ops_, W=BK(obk) + [B_ot[sl]])
                    vop(POOL, "tensor_tensor", R=[B_ot[sl], B_acc], J=[B_acc], out=dst,
                        in0=Otmp[sl].rearrange("p (a t) -> p a t", a=2), in1=dst, op=ALU.add)

            def emit_FINa(ti):
                if not (0 <= ti < len(tl)):
                    return
                pi, r, a, ph = tl[ti]
                span = 128 * r
                qs = a * span + ph
                if pi == 2:
                    qsl_ = slice(qs, qs + 127 * r + 1, r)
                    vop(DVE, "reciprocal", R=[B_acc], J=[B_fin], out=acc3[:, 1, qsl_], in_=acc3[:, 1, qsl_])
                    vop(DVE, "tensor_tensor", R=[B_acc, B_fin], J=[B_fin], out=acc3[:, 0, qsl_], in0=acc3[:, 0, qsl_],
                        in1=acc3[:, 1, qsl_], op=ALU.mult)

            def emit_FINb(ti):
                if not (0 <= ti < len(tl)):
                    return
                pi, r, a, ph = tl[ti]
                span = 128 * r
                qs = a * span + ph
                if pi == 2:
                    qsl_ = slice(qs, qs + 127 * r + 1, r)
                    fs_ = state.get("first_stg", True)
                    vop(POOL, "tensor_tensor", R=[B_acc, B_fin, B_zT], W=[B_astg] if fs_ else [], J=[] if fs_ else [B_astg],
                        out=attn_stg[:, qsl_], in0=acc3[:, 0, qsl_], in1=zT[:, qsl_], op=ALU.mult)
                    state["first_stg"] = False

            NTL = len(tl)
            emit_S(0)
            yield 1
            emit_S(1)
            yield 2
            for ti in range(1, NTL + 1):
                emit_FINb(ti - 5)
                emit_FINa(ti - 4)
                emit_PV(ti - 1)
                if ti + 1 < NTL:
                    emit_S(ti + 1)
                if tl[min(ti, NTL - 1)][0] == 2:
                    yield 3
                else:
                    yield (2 if ti % 2 == 0 else 1)
            emit_FINb(NTL - 4)
            for ti in range(NTL - 3, NTL):
                emit_FINa(ti)
            yield 1
            for ti in range(NTL - 3, NTL):
                emit_FINb(ti)
            yield 1
            dma(SP, s_attn[h], attn_stg, "astg", R=[B_astg], J=[B_sattn])
            yield 1

        def interleave(*gens):
            gens = list(gens)
            while gens:
                for g in list(gens):
                    try:
                        next(g)
                    except StopIteration:
                        gens.remove(g)

        pj_set[:] = [0, 1, 2]
        slotA = {0: load_w("A0"), 1: load_w("A1")}
        interleave(projA(0, setsA[0]))
        for h in range(16):
            if h + 2 < 16:
                slotA[h + 2] = load_w("A%d" % (h + 2))
            gattn = attnA(h, setsA[h % 2])
            gproj = projA(h + 1, setsA[(h + 1) % 2]) if h + 1 < 16 else None
            while gattn is not None or gproj is not None:
                npull = 4
                if gattn is not None:
                    try:
                        npull = next(gattn) or 1
                    except StopIteration:
                        gattn = None
                for _ in range(npull):
                    if gproj is not None:
                        try:
                            next(gproj)
                        except StopIteration:
                            gproj = None
        pj_set[:] = [0, 1, 2, 3]
        fence(allA)

        carve.reset()
        MB = carve_mlstm(False)
        gate_math(MB["gs"], MB["B_gs"])
        wload_rr[0] = 0
        run_units(MB, 8)
        fence(MB["all"])


        carve.reset()
        mergedT = carve.b16(16 * 2048)
        merged3 = mergedT.rearrange("p (k t) -> p k t", k=16)
        gst = [carve.b16(2048) for _ in range(2)]
        tmpm = [carve.b16(512) for _ in range(2)]
        B_merged = Buf("mergedT")
        B_gst = [Buf("gst0"), Buf("gst1")]
        B_tmpm = [Buf("tmpm0"), Buf("tmpm1")]
        allC = [B_merged] + B_gst + B_tmpm
        touch(allC, R=[B_arena])
        B_sg = Buf("s_g")
        gi = 0
        for nm, dst in (("GA", s_ga), ("GM", s_gm)):
            for cbk in range(4):
                s = load_w("%s%d" % (nm, cbk))
                for ct in range(4):
                    dcol = cbk * 4 + ct
                    sl = gi % 2
                    gi += 1

                    def ev_g(tt, bk, sl=sl):
                        act(gst[sl][:, tt * 512:(tt + 1) * 512], banks[bk][:, :], AF.Sigmoid,
                            W=BK(bk) + ([B_gst[sl]] if tt == 0 else []), J=[] if tt == 0 else [B_gst[sl]])
                    proj_fm(s, ct * 128, ev_g)
                    dma(SP, dst[dcol], gst[sl], "gst%d" % sl, R=[B_gst[sl]], J=[B_sg])

        for branch, (nm, src, Bsrc, gsrc) in enumerate((("WA", s_attn, B_sattn, s_ga), ("WM", s_mem, B_smem, s_gm))):
            for k in range(16):
                dma(SP, R0[:, k, :], src[k], "R0l", R=[Bsrc], W=[B_R0] if k == 0 else [], J=[] if k == 0 else [B_R0])
            for cbk in range(4):
                s = load_w("%s%d" % (nm, cbk))
                for ct in range(4):
                    dcol = cbk * 4 + ct
                    sl = gi % 2
                    gi += 1
                    dma(SP, gst[sl], gsrc[dcol], "gld%d" % sl, R=[B_sg], W=[B_gst[sl]])

                    def ev_y(tt, bk, sl=sl, dcol=dcol, branch=branch):
                        dst = merged3[:, dcol, tt * 512:(tt + 1) * 512]
                        if branch == 0:
                            vop(DVE, "tensor_tensor", R=[B_gst[sl]], W=BK(bk), J=[B_merged], out=dst, in0=banks[bk][:, :],
                                in1=gst[sl][:, tt * 512:(tt + 1) * 512], op=ALU.mult)
                        else:
                            ts_ = tt % 2
                            vop(DVE, "tensor_tensor", R=[B_gst[sl]], W=BK(bk) + [B_tmpm[ts_]], out=tmpm[ts_], in0=banks[bk][:, :],
                                in1=gst[sl][:, tt * 512:(tt + 1) * 512], op=ALU.mult)
                            vop(DVE, "tensor_tensor", R=[B_tmpm[ts_], B_merged], J=[B_merged], out=dst, in0=dst, in1=tmpm[ts_], op=ALU.add)
                    proj_fm(s, ct * 128, ev_y)

        for cbk in range(4):
            off, c = _BLK_OFF["WO%d" % cbk]
            srcw = wst[:, off:off + 16 * c].rearrange("p (k c) -> p k c", c=c)
            dma(POOL, R0[:, :, cbk * 512:(cbk + 1) * 512], srcw, "R0w", W=[B_R0] if cbk == 0 else [], J=[] if cbk == 0 else [B_R0],
                max_dma_last_dim=8192)
        w0f = wsl_t[0].bitcast(F32)
        w1f = wsl_t[1].bitcast(F32)
        res = [w0f[:, 0:8, :].rearrange("p a b -> p (a b)"), w0f[:, 8:16, :].rearrange("p a b -> p (a b)")]
        fg_rep = w1f[:, 0:8, :].rearrange("p a b -> p (a b)")
        junkf = wsl_t[1][:, 8:12, :].rearrange("p a b -> p (a b)")
        B_res = [Buf("res0"), Buf("res1")]
        B_fg = Buf("fg_rep")
        B_junkf = Buf("junkf")
        B_outs = Buf("outs")
        touch([B_w[0], B_res[0], B_res[1]], R=[B_w[0]])
        touch([B_w[1], B_fg, B_junkf], R=[B_w[1]])
        dma(SP, fg_rep, fg_rep_d, "fg", W=[B_fg])
        for j in range(NT):
            sl = j % 2
            dma(SP, res[sl], x_own[j * 128:(j + 1) * 128, :], "res%d" % sl, W=[B_res[sl]])
            for cg in range(4):
                bk = next_pj()
                for k in range(16):
                    mm(banks[bk][:, :], merged3[:, k, j * 128:(j + 1) * 128], R0[:, k, cg * 512:(cg + 1) * 512], k == 0, k == 15,
                       bk, R=[B_merged, B_R0])
                vop(DVE, "tensor_tensor", R=[B_res[sl]], W=BK(bk), J=[B_res[sl]], out=res[sl][:, cg * 512:(cg + 1) * 512],
                    in0=banks[bk][:, :], in1=res[sl][:, cg * 512:(cg + 1) * 512], op=ALU.add)
            sc_ = stat[:, 56 + 3 * sl:59 + 3 * sl]
            act(junkf, res[sl], AF.Square, R=[B_res[sl]], W=[B_junkf], J=[B_stat], accum_out=sc_[:, 0:1])
            act(sc_[:, 1:2], sc_[:, 0:1], AF.Sqrt, R=[B_stat], J=[B_stat], scale=1.0 / D, bias=EPS)
            vop(DVE, "reciprocal", R=[B_stat], J=[B_stat], out=sc_[:, 2:3], in_=sc_[:, 1:2])
            vop(DVE, "scalar_tensor_tensor", R=[B_stat, B_fg], W=[B_res[sl]], out=res[sl], in0=res[sl],
                scalar=sc_[:, 2:3], in1=fg_rep, op0=ALU.mult, op1=ALU.mult)
            o = dma(SP, out_d[j * 128:(j + 1) * 128, :], res[sl], "out%d" % sl, R=[B_res[sl]], J=[B_outs])
            P.final_waits.append(o)
        if dbg:
            P.final_waits.extend(B_sattn.writers)
            P.final_waits.extend(B_smem.writers)

        sems = []

        def sem_alloc(name):
            sm = st.enter_context(nc.semaphore(name))
            sems.append(sm)
            return sm
        P.emit(sem_alloc)
        nc._n_ops = {e: len(P.ops[e]) for e in ENGS}
        nc._n_sems = len(sems)
    return nc


_CACHE = {}


def _prep_inputs(x, norm_g, w_in, b_if, conv_w, conv_b, mlstm_norm_g, w_attn_branch, w_mlstm_branch, w_out, final_norm_g):
    f32 = np.float32
    x = np.asarray(x, f32)
    wstream = _build_wstream(np.asarray(w_in, f32), np.asarray(w_attn_branch, f32), np.asarray(w_mlstm_branch, f32), np.asarray(w_out, f32))
    g_rep = np.ascontiguousarray(np.broadcast_to(np.asarray(norm_g, f32)[None, :], (128, D)))
    fg_rep = np.ascontiguousarray(np.broadcast_to(np.asarray(final_norm_g, f32)[None, :], (128, D)))
    gm_fm = np.ascontiguousarray(np.asarray(mlstm_norm_g, f32).reshape(16, 128).T)
    cwt = np.asarray(conv_w, f32)
    cw = np.ascontiguousarray(cwt.reshape(4, 16, 128).transpose(2, 1, 0).reshape(128, 64))
    cb = np.ascontiguousarray(np.asarray(conv_b, f32).reshape(16, 128).T)
    bif = np.ascontiguousarray(np.broadcast_to(np.tile(np.asarray(b_if, f32), 16)[None, :], (128, 256)))
    in_maps = []
    zeros = np.zeros((TOK, D), f32)
    for c in range(8):
        b, hh = c // 2, c % 2
        flag = np.full((128, 2), float(hh), f32)
        in_maps.append({
            "x_own": np.ascontiguousarray(x[b, hh * TOK:(hh + 1) * TOK]),
            "x_pre": np.ascontiguousarray(x[b, 0:TOK]) if hh == 1 else zeros,
            "wst": wstream, "g_rep": g_rep, "fg_rep": fg_rep, "gm_fm": gm_fm, "cw": cw, "cb": cb, "bif": bif, "flag": flag,
        })
    return in_maps


def kernel(x, norm_g, w_in, b_if, conv_w, conv_b, mlstm_norm_g, w_attn_branch, w_mlstm_branch, w_out, final_norm_g):
    in_maps = _prep_inputs(x, norm_g, w_in, b_if, conv_w, conv_b, mlstm_norm_g, w_attn_branch, w_mlstm_branch, w_out, final_norm_g)
    if "nc" not in _CACHE:
        _CACHE["nc"] = build_nc()
    nc = _CACHE["nc"]
    res = run_bass_kernel_spmd(nc, in_maps, core_ids=list(range(8)))
    out = np.empty((4, 4096, D), np.float32)
    for c in range(8):
        b, hh = c // 2, c % 2
        out[b, hh * TOK:(hh + 1) * TOK] = res.results[c]["out"]
    return out
```

```python
import contextlib
import numpy as np
import concourse.bass as bass
import concourse.mybir as mybir
from concourse.bass_utils import run_bass_kernel_spmd

F32 = mybir.dt.float32
BF16 = mybir.dt.bfloat16
I32 = mybir.dt.int32
AF = mybir.ActivationFunctionType
ALU = mybir.AluOpType

PE, ACT, DVE, POOL, SP = "pe", "act", "dve", "pool", "sp"
ENGS = (PE, ACT, DVE, POOL, SP)

D = 2048
TOK = 2048
NT = 16
EPS = 1e-6
QSCALE = 128.0 ** -0.5
NEG = -30000.0


class Buf:
    __slots__ = ("name", "writers", "readers", "war", "dsem", "dcount")

    def __init__(self, name):
        self.name = name
        self.writers = []
        self.readers = []
        self.war = []
        self.dsem = None
        self.dcount = 0


class Op:
    __slots__ = ("eng", "fn", "deps", "is_dma", "sig", "semval", "dbuf", "idx")

    def __init__(self, eng, fn, is_dma=False):
        self.eng = eng
        self.fn = fn
        self.deps = []
        self.is_dma = is_dma
        self.sig = False
        self.semval = None
        self.dbuf = None
        self.idx = -1


def _compress(ops):
    last = {}
    for d in ops:
        k = ("dma", id(d.dbuf)) if d.is_dma else d.eng
        if k not in last or last[k].idx < d.idx:
            last[k] = d
    return list(last.values())


class Prog:
    def __init__(self, nc):
        self.nc = nc
        self.ops = {e: [] for e in ENGS}
        self.all_ops = []
        self.final_waits = []
        self.nsem = 0

    def op(self, eng, fn, reads=(), writes=(), joins=(), is_dma=False, dma_buf=None):
        o = Op(eng, fn, is_dma)
        deps = []
        for b in reads:
            deps.extend(b.writers)
        for b in writes:
            deps.extend(b.writers)
            deps.extend(b.readers)
        for b in joins:
            deps.extend(b.readers)
            deps.extend(b.war)
        o.idx = len(self.ops[eng])
        if is_dma:
            o.dbuf = dma_buf
        for b in reads:
            b.readers.append(o)
            if len(b.readers) > 16:
                b.readers = _compress(b.readers)
        for b in writes:
            b.war = _compress(b.writers + b.readers)
            b.writers = [o]
            b.readers = []
        for b in joins:
            b.writers.append(o)
            if len(b.writers) > 16:
                b.writers = _compress(b.writers)
        o.deps = _compress([d for d in deps if d is not o])
        self.ops[eng].append(o)
        self.all_ops.append(o)
        return o

    def emit(self, sem_alloc):
        for o in self.all_ops:
            for d in o.deps:
                if d.is_dma:
                    d.sig = True
                elif d.eng == PE and o.eng == PE and not o.is_dma:
                    continue
                else:
                    d.sig = True
        for o in self.final_waits:
            o.sig = True
        for e in ENGS:
            cnt = 0
            sem = None
            for o in self.ops[e]:
                if o.is_dma:
                    b = o.dbuf
                    if b.dsem is None:
                        b.dsem = sem_alloc("d_" + b.name)
                    b.dcount += 16
                    o.semval = (b.dsem, b.dcount)
                elif o.sig:
                    if sem is None or cnt >= 30000:
                        self.nsem += 1
                        sem = sem_alloc("e_%s%d" % (e, self.nsem))
                        cnt = 0
                    cnt += 1
                    o.semval = (sem, cnt)
        prog = self

        def run(eng_name, eng):
            waited = {}
            for o in prog.ops[eng_name]:
                need = {}
                for d in o.deps:
                    if (not d.is_dma) and d.eng == PE and eng_name == PE and not o.is_dma:
                        continue
                    s, v = d.semval
                    k = id(s)
                    if k not in need or need[k][1] < v:
                        need[k] = (s, v)
                for k, (s, v) in need.items():
                    if waited.get(k, 0) >= v:
                        continue
                    eng.wait_ge(s, v)
                    waited[k] = v
                ins = o.fn(eng)
                if o.is_dma:
                    ins.then_inc(o.semval[0], 16)
                elif o.sig:
                    ins.then_inc(o.semval[0], 1)
            if eng_name == SP:
                need = {}
                for d in prog.final_waits:
                    s, v = d.semval
                    k = id(s)
                    if k not in need or need[k][1] < v:
                        need[k] = (s, v)
                for k, (s, v) in need.items():
                    eng.wait_ge(s, v)

        with self.nc.Block() as block:
            @block.tensor
            def _(e):
                run(PE, e)

            @block.scalar
            def _(e):
                run(ACT, e)

            @block.vector
            def _(e):
                run(DVE, e)

            @block.gpsimd
            def _(e):
                run(POOL, e)

            @block.sync
            def _(e):
                run(SP, e)


O_AQ, O_AK, O_AV, O_AZ = 0, 2048, 4096, 6144
O_MQK, O_MV = 8192, 10240
O_MI, O_MF = 12288, 12296
O_MO, O_MZ = 12304, 14352
O_GA, O_GM = 16400, 18448


def _block_columns():
    blocks = []
    for h in range(16):
        cols = np.concatenate([np.arange(o + 128 * h, o + 128 * h + 128) for o in (O_AQ, O_AK, O_AV, O_AZ)])
        blocks.append(("A%d" % h, "w_in", cols))
    for h in range(8):
        cols = np.concatenate([
            np.arange(O_MQK + 128 * h, O_MQK + 128 * h + 128),
            np.arange(O_MQK + 1024 + 128 * h, O_MQK + 1024 + 128 * h + 128),
            np.arange(O_MV + 256 * h, O_MV + 256 * h + 256)])
        blocks.append(("B1_%d" % h, "w_in", cols))
        cols = np.concatenate([
            np.arange(O_MO + 256 * h, O_MO + 256 * h + 256),
            np.arange(O_MZ + 256 * h, O_MZ + 256 * h + 256)])
        blocks.append(("B2_%d" % h, "w_in", cols))
    blocks.append(("G", "w_in", np.arange(O_MI, O_MI + 16)))
    for i in range(4):
        blocks.append(("GA%d" % i, "w_in", np.arange(O_GA + 512 * i, O_GA + 512 * i + 512)))
    for i in range(4):
        blocks.append(("GM%d" % i, "w_in", np.arange(O_GM + 512 * i, O_GM + 512 * i + 512)))
    for nm, src in (("WA", "w_attn_branch"), ("WM", "w_mlstm_branch"), ("WO", "w_out")):
        for i in range(4):
            blocks.append(("%s%d" % (nm, i), src, np.arange(512 * i, 512 * i + 512)))
    return blocks


_BLOCKS = _block_columns()
_BLK_OFF = {}
_off = 0
for _nm, _src, _cols in _BLOCKS:
    _BLK_OFF[_nm] = (_off, len(_cols))
    _off += 16 * len(_cols)
WST_COLS = _off


def _build_wstream(w_in, w_a, w_m, w_o):
    srcs = {"w_in": w_in, "w_attn_branch": w_a, "w_mlstm_branch": w_m, "w_out": w_o}
    out = np.empty((128, WST_COLS), dtype=np.float32)
    for nm, src, cols in _BLOCKS:
        off, c = _BLK_OFF[nm]
        blk = srcs[src][:, cols]
        blk = blk.reshape(16, 128, c).transpose(1, 0, 2)
        out[:, off:off + 16 * c] = blk.reshape(128, 16 * c)
    return out


def build_nc(dbg=False):
    nc = bass.Bass("TRN2", target_bir_lowering=False)
    x_own = nc.dram_tensor("x_own", [TOK, D], F32, kind="ExternalInput").ap()
    x_pre = nc.dram_tensor("x_pre", [TOK, D], F32, kind="ExternalInput").ap()
    wst = nc.dram_tensor("wst", [128, WST_COLS], F32, kind="ExternalInput").ap()
    g_rep_d = nc.dram_tensor("g_rep", [128, D], F32, kind="ExternalInput").ap()
    fg_rep_d = nc.dram_tensor("fg_rep", [128, D], F32, kind="ExternalInput").ap()
    gm_fm_d = nc.dram_tensor("gm_fm", [128, 16], F32, kind="ExternalInput").ap()
    cw_d = nc.dram_tensor("cw", [128, 64], F32, kind="ExternalInput").ap()
    cb_d = nc.dram_tensor("cb", [128, 16], F32, kind="ExternalInput").ap()
    bif_d = nc.dram_tensor("bif", [128, 256], F32, kind="ExternalInput").ap()
    flag_d = nc.dram_tensor("flag", [128, 2], F32, kind="ExternalInput").ap()
    out_d = nc.dram_tensor("out", [TOK, D], F32, kind="ExternalOutput").ap()
    skind = "ExternalOutput" if dbg else "Internal"
    s_kT = nc.dram_tensor("s_kT", [16, 128, TOK], BF16, kind="Internal").ap()
    s_vT = nc.dram_tensor("s_vT", [16, 128, TOK], BF16, kind="Internal").ap()
    s_attn = nc.dram_tensor("s_attn", [16, 128, TOK], BF16, kind=skind).ap()
    s_mem = nc.dram_tensor("s_mem", [16, 128, TOK], BF16, kind=skind).ap()
    s_ga = nc.dram_tensor("s_ga", [16, 128, TOK], BF16, kind="Internal").ap()
    s_gm = nc.dram_tensor("s_gm", [16, 128, TOK], BF16, kind="Internal").ap()

    with contextlib.ExitStack() as st:
        def sb(name, shape, dt):
            return st.enter_context(nc.sbuf_tensor(name, shape, dt))

        def psum(name, shape, dt):
            return st.enter_context(nc.psum_tensor(name, shape, dt))

        P = Prog(nc)

        R0 = sb("R0", [128, 16, 2048], BF16)
        B_R0 = Buf("R0")
        wsl_t = [sb("wsl%d" % i, [128, 16, 512], BF16) for i in range(2)]
        B_w = [Buf("wsl%d" % i) for i in range(2)]
        ARENA_BYTES = 94 * 1024
        arena = sb("arena", [128, ARENA_BYTES // 2], BF16)
        arena32 = arena.bitcast(F32)
        ident = sb("ident", [128, 128], BF16)
        onesb = sb("onesb", [128, 128], BF16)
        flagones = sb("flagones", [128, 128], BF16)
        cf = sb("cf", [128, 1536], F32)
        ci = sb("ci", [128, 256], I32)
        distPC = cf[:, 0:256]
        maskPC = cf[:, 256:512]
        m01f = cf[:, 512:640]
        onesf = cf[:, 640:768]
        gm_fm = cf[:, 768:784]
        cw = cf[:, 784:848]
        cb = cf[:, 848:864]
        bif = cf[:, 864:1120]
        flag = cf[:, 1120:1122]
        idf = cf[:, 1152:1280]
        B_const = Buf("const")
        cstate = sb("cstate", [128, 8, 257], F32)
        tails = sb("tails", [128, 16, 3], F32)
        B_cstate = [Buf("cstate%d" % h) for h in range(8)]
        B_tails = Buf("tails")
        stat = sb("stat", [128, 64], F32)
        B_stat = Buf("stat")

        banks = [psum("bank%d" % i, [128, 512], F32) for i in range(8)]
        banks_b = [b.bitcast(BF16) for b in banks]
        B_bank = [Buf("bank%d" % i) for i in range(8)]

        def BK(i):
            return [B_bank[i]]

        pj_rr = [0]
        pj_set = [0, 1, 2, 3]

        def next_pj():
            i = pj_set[pj_rr[0] % len(pj_set)]
            pj_rr[0] += 1
            return i

        G = {}

        def grp(name):
            if name not in G:
                G[name] = Buf("g_" + name)
            return G[name]

        class Carver:
            def __init__(self):
                self.off = 0

            def reset(self):
                self.off = 0

            def b16(self, n):
                self.off = (self.off + 3) // 4 * 4
                o = self.off
                self.off += n * 2
                assert self.off <= ARENA_BYTES, self.off
                return arena[:, o // 2:o // 2 + n]

            def f32(self, n):
                self.off = (self.off + 3) // 4 * 4
                o = self.off
                self.off += n * 4
                assert self.off <= ARENA_BYTES, self.off
                return arena32[:, o // 4:o // 4 + n]

        carve = Carver()
        B_arena = Buf("arena_phase")
        B_smem = Buf("s_mem")

        def dma(eng, out, in_, g, R=(), W=(), J=(), **kw):
            return P.op(eng, lambda e, o=out, i=in_, kw=kw: e.dma_start(out=o, in_=i, **kw),
                        reads=R, writes=W, joins=J, is_dma=True, dma_buf=grp(g))

        def mm(out, lhsT, rhs, start, stop, bk, R=()):
            return P.op(PE, lambda e, o=out, l=lhsT, r=rhs, s=start, t=stop:
                        e.matmul(o, lhsT=l, rhs=r, start=s, stop=t), reads=R, writes=BK(bk))

        def tr(out, in_, bk, R=()):
            return P.op(PE, lambda e, o=out, i=in_: e.transpose(out=o, in_=i, identity=ident[:]),
                        reads=list(R) + [B_const], writes=BK(bk))

        def act(out, in_, func, R=(), W=(), J=(), **kw):
            return P.op(ACT, lambda e, o=out, i=in_, f=func, kw=kw: e.activation(out=o, in_=i, func=f, **kw),
                        reads=R, writes=W, joins=J)

        def vop(eng, name, R=(), W=(), J=(), **kw):
            return P.op(eng, lambda e, n=name, kw=kw: getattr(e, n)(**kw), reads=R, writes=W, joins=J)

        def cp(eng, out, in_, R=(), W=(), J=()):
            if eng == ACT:
                return vop(ACT, "copy", R=R, W=W, J=J, out=out, in_=in_)
            return vop(eng, "tensor_copy", R=R, W=W, J=J, out=out, in_=in_)

        B_touch = Buf("touch")

        def touch(bufs, R=()):
            P.op(DVE, lambda e: e.memset(stat[:, 63:64], 0.0), reads=list(R), writes=list(bufs) + [B_touch])

        wload_rr = [0]

        def load_w(name, slot=None):
            off, c = _BLK_OFF[name]
            if slot is None:
                s = wload_rr[0] % 2
                wload_rr[0] += 1
            else:
                s = slot
            src = wst[:, off:off + 16 * c].rearrange("p (k c) -> p k c", c=c)
            dma(POOL, wsl_t[s][:, :, 0:c], src, "wsl%d" % s, W=[B_w[s]], max_dma_last_dim=8192)
            return s

        dma(SP, gm_fm, gm_fm_d, "const", J=[B_const])
        dma(SP, cw, cw_d, "const", J=[B_const])
        dma(SP, cb, cb_d, "const", J=[B_const])
        dma(SP, bif, bif_d, "const", J=[B_const])
        dma(SP, flag, flag_d, "const", J=[B_const])
        B_ci = Buf("ci")
        vop(POOL, "iota", W=[B_ci], out=ci[:, 0:128], pattern=[[1, 128]], base=128, channel_multiplier=-1)
        vop(POOL, "iota", R=[B_ci], W=[B_ci], out=ci[:, 128:256], pattern=[[1, 128]], base=0, channel_multiplier=-1)
        vop(POOL, "tensor_copy", R=[B_ci], W=[B_const], out=distPC, in_=ci[:, :])
        vop(POOL, "memset", R=[B_const], W=[B_const], ap=maskPC, constant=0.0)
        vop(POOL, "memset", R=[B_const], W=[B_const], ap=onesf, constant=1.0)
        vop(POOL, "affine_select", R=[B_const], W=[B_const], out=maskPC[:, 0:128], in_=maskPC[:, 0:128],
            pattern=[[-1, 128]], compare_op=ALU.is_ge, fill=NEG, base=0, channel_multiplier=1)
        vop(POOL, "affine_select", R=[B_const], W=[B_const], out=maskPC[:, 128:256], in_=maskPC[:, 128:256],
            pattern=[[1, 128]], compare_op=ALU.is_ge, fill=NEG, base=0, channel_multiplier=-1)
        vop(POOL, "affine_select", R=[B_const], W=[B_const], out=m01f, in_=onesf,
            pattern=[[1, 128]], compare_op=ALU.is_ge, fill=0.0, base=0, channel_multiplier=-1)
        vop(POOL, "affine_select", R=[B_const], W=[B_const], out=idf, in_=onesf,
            pattern=[[-1, 128]], compare_op=ALU.is_equal, fill=0.0, base=0, channel_multiplier=1)
        vop(POOL, "tensor_copy", R=[B_const], W=[B_const], out=ident[:], in_=idf)
        vop(POOL, "tensor_copy", R=[B_const], W=[B_const], out=onesb[:], in_=onesf)
        vop(DVE, "tensor_scalar", R=[B_const], W=[B_const], out=flagones[:], in0=onesf,
            scalar1=flag[:, 0:1], scalar2=None, op0=ALU.mult)
        vop(DVE, "memset", W=[B_tails], ap=tails[:], constant=0.0)
        for h in range(8):
            vop(DVE, "memset", W=[B_cstate[h]], ap=cstate[:, h, :], constant=0.0)
        vop(DVE, "memset", R=[B_const], W=[B_const], ap=stat[:, 62:63], constant=float(np.log(QSCALE)))
        CONSTS = [B_const]

        def phase_norm(x_src):
            carve.reset()
            xt = [carve.f32(2048) for _ in range(4)]
            g_rep = carve.f32(2048)
            xn = [carve.b16(2048) for _ in range(2)]
            junk = carve.b16(2048)
            B_xt = [Buf("xt%d" % i) for i in range(4)]
            B_xn = [Buf("xn0"), Buf("xn1")]
            B_g = Buf("g_rep")
            B_junk = Buf("junk")
            users = B_xt + B_xn + [B_g, B_junk]
            touch(users, R=[B_arena])
            dma(SP, g_rep, g_rep_d, "g_rep", W=[B_g])
            first_r0 = [True]
            B_st = [Buf("nst%d" % i) for i in range(4)]

            def stage1(j):
                s = j % 4
                s2_ = j % 2
                bs_ = B_st[j % 4]
                dma(SP, xt[s], x_src[j * 128:(j + 1) * 128, :], "xt%d" % s, W=[B_xt[s]])
                act(junk, xt[s], AF.Square, R=[B_xt[s]], W=[B_junk, bs_], accum_out=stat[:, j:j + 1])
                act(stat[:, 16 + j:17 + j], stat[:, j:j + 1], AF.Sqrt, R=[bs_], W=[bs_], scale=1.0 / D, bias=EPS)
                vop(DVE, "reciprocal", R=[bs_], W=[bs_], out=stat[:, 32 + j:33 + j], in_=stat[:, 16 + j:17 + j])
                vop(DVE, "scalar_tensor_tensor", R=[bs_, B_xt[s], B_g], W=[B_xn[s2_]], out=xn[s2_], in0=xt[s],
                    scalar=stat[:, 32 + j:33 + j], in1=g_rep, op0=ALU.mult, op1=ALU.mult)

            def stage2(j):
                s2_ = j % 2
                for q in range(4):
                    bk = next_pj()
                    for i in range(4):
                        kk = 4 * q + i
                        tr(banks_b[bk][:, i * 128:(i + 1) * 128], xn[s2_][:, kk * 128:(kk + 1) * 128], bk, R=[B_xn[s2_]])
                    src = banks_b[bk][:, 0:512].rearrange("p (a b) -> p a b", b=128)
                    dst = R0[:, 4 * q:4 * q + 4, j * 128:(j + 1) * 128]
                    cp(ACT if q % 2 == 0 else DVE, dst, src, W=BK(bk) + ([B_R0] if first_r0[0] else []), J=[] if first_r0[0] else [B_R0])
                    first_r0[0] = False

            stage1(0)
            for j in range(NT):
                if j + 1 < NT:
                    stage1(j + 1)
                stage2(j)
            return users

        def fence(bufs):
            P.op(DVE, lambda e: e.memset(stat[:, 63:64], 0.0), reads=list(bufs), writes=[B_arena, B_touch] + list(bufs))

        def proj_fm(s, c0, evac):
            for tt in range(4):
                bk = next_pj()
                for k in range(16):
                    mm(banks[bk][:, :], wsl_t[s][:, k, c0:c0 + 128], R0[:, k, tt * 512:(tt + 1) * 512],
                       k == 0, k == 15, bk, R=[B_w[s], B_R0])
                evac(tt, bk)

        def proj_tm(s, c0, ncols, j, bk, bc0):
            for k in range(16):
                mm(banks[bk][:, bc0:bc0 + ncols], R0[:, k, j * 128:(j + 1) * 128], wsl_t[s][:, k, c0:c0 + ncols],
                   k == 0, k == 15, bk, R=[B_w[s], B_R0])

        def gate_math(gs, B_gs):
            s = load_w("G")
            bk = next_pj()
            for j in range(NT):
                proj_tm(s, 0, 16, j, bk, j * 16)
            vop(DVE, "tensor_tensor", R=CONSTS, W=BK(bk) + [B_gs], out=gs["gpre"], in0=banks[bk][:, 0:256], in1=bif, op=ALU.add)
            gp3 = gs["gpre"].rearrange("p (j c) -> p j c", c=16)
            ig3 = gp3[:, :, 0:8]
            fp3 = gp3[:, :, 8:16]
            lf3 = gs["lf"].rearrange("p (j c) -> p j c", c=8)
            act(lf3, fp3, AF.Exp, R=[B_gs], W=[B_gs], scale=-1.0)
            act(gs["lf"], gs["lf"], AF.Ln, R=[B_gs], W=[B_gs], bias=1.0)
            vop(DVE, "tensor_scalar", R=[B_gs], W=[B_gs], out=gs["lf"], in0=gs["lf"], scalar1=-1.0, scalar2=None, op0=ALU.mult)
            bk2 = next_pj()
            mm(banks[bk2][:, 0:128], m01f, gs["lf"], True, True, bk2, R=[B_gs] + CONSTS)
            mm(banks[bk2][:, 128:256], onesf, gs["lf"], True, True, bk2, R=[B_gs] + CONSTS)
            d3 = gs["d"].rearrange("p (j c) -> p j c", c=8)
            vop(DVE, "tensor_tensor", R=[B_gs], W=BK(bk2) + [B_gs], out=d3, in0=ig3,
                in1=banks[bk2][:, 0:128].rearrange("p (j c) -> p j c", c=8), op=ALU.subtract)
            act(gs["u"], gs["d"], AF.Exp, R=[B_gs] + CONSTS, W=[B_gs], bias=stat[:, 62:63])
            vop(DVE, "tensor_tensor", R=[B_gs], W=BK(bk2) + [B_gs], out=gs["d2"], in0=gs["d"], in1=banks[bk2][:, 128:256], op=ALU.add)
            act(gs["w"], gs["d2"], AF.Exp, R=[B_gs] + CONSTS, W=[B_gs], bias=stat[:, 62:63])
            act(gs["ec"], banks[bk2][:, 128:256], AF.Exp, R=[B_gs], W=BK(bk2) + [B_gs])
            act(gs["emb"], banks[bk2][:, 0:128], AF.Exp, R=[B_gs], W=BK(bk2) + [B_gs], scale=-1.0)

        def carve_gates():
            gs = {"gpre": carve.f32(256)}
            for n in ("lf", "d", "d2", "u", "w", "ec", "emb"):
                gs[n] = carve.f32(128)
            return gs

        def conv_silu(s, c0, coltile, tail_idx, upre, ybuf, outT, B_u, B_y, B_out, save_tail):
            vop(DVE, "tensor_copy", R=[B_tails], W=[B_u], out=upre[:, 0:3], in_=tails[:, tail_idx, :])

            def ev(tt, bk):
                cp(ACT, upre[:, 3 + tt * 512:3 + (tt + 1) * 512], banks[bk][:, :], W=BK(bk), J=[B_u])
            proj_fm(s, c0, ev)
            if save_tail:
                vop(DVE, "tensor_copy", R=[B_u], W=[B_tails], out=tails[:, tail_idx, :], in_=upre[:, 2048:2051])
            vop(DVE, "tensor_scalar", R=[B_u] + CONSTS, W=[B_y], out=ybuf, in0=upre[:, 0:2048],
                scalar1=cw[:, coltile * 4:coltile * 4 + 1], scalar2=cb[:, coltile:coltile + 1], op0=ALU.mult, op1=ALU.add)
            for tp in range(1, 4):
                vop(DVE, "scalar_tensor_tensor", R=[B_u] + CONSTS, W=[B_y], out=ybuf, in0=upre[:, tp:tp + 2048],
                    scalar=cw[:, coltile * 4 + tp:coltile * 4 + tp + 1], in1=ybuf, op0=ALU.mult, op1=ALU.add)
            act(outT, ybuf, AF.Silu, R=[B_y], W=[B_out])

        def interleave(*gens):
            gens = list(gens)
            while gens:
                for g in list(gens):
                    try:
                        next(g)
                    except StopIteration:
                        gens.remove(g)

        HT = 1024
        HJ = 8

        def carve_mlstm(prefix):
            M = {"prefix": prefix}
            M["gs"] = carve_gates()
            M["B_gs"] = Buf("gs")
            M["upre"] = carve.f32(HT + 4)
            M["ybuf"] = carve.f32(HT)
            M["B_u"], M["B_y"] = Buf("upre"), Buf("ybuf")
            M["vpp"] = [carve.b16(258) for _ in range(2)]
            M["B_vpp"] = [Buf("vpp0"), Buf("vpp1")]
            sets = []
            for i_ in range(2):
                d_ = dict(i=i_, kT=carve.b16(HT), ktok=carve.b16(HJ * 128), vaug=carve.b16(HJ * 257 + 1),
                          B_kT=Buf("kTm%d" % i_), B_ktok=Buf("ktok%d" % i_), B_vaug=Buf("vaug%d" % i_))
                if not prefix:
                    d_.update(qT=carve.b16(HT), sigo=carve.b16(HJ * 256), zT=carve.b16(2 * HT),
                              B_qT=Buf("qTm%d" % i_), B_sigo=Buf("sigo%d" % i_), B_zT=Buf("zTm%d" % i_))
                sets.append(d_)
            M["sets"] = sets
            allb = [M["B_gs"], M["B_u"], M["B_y"]] + M["B_vpp"]
            for d_ in sets:
                allb += [v for k_, v in d_.items() if k_.startswith("B_")]
            if not prefix:
                M["cellf"] = [carve.f32(256) for _ in range(2)]
                M["Cb"] = carve.b16((HJ + 1) * 258)
                M["scT"] = [carve.b16(128) for _ in range(2)]
                M["vp1"] = [carve.b16(258) for _ in range(4)]
                M["celln"] = [carve.b16(256) for _ in range(2)]
                M["mem_stg"] = carve.b16(2 * HT)
                M["junk256"] = carve.b16(256)
                M["B_cell"] = [Buf("cell0"), Buf("cell1")]
                M["B_Cb"] = [Buf("Cb0"), Buf("Cb1")]
                M["B_sc"] = [Buf("sc0"), Buf("sc1")]
                M["B_vp1"] = [Buf("vp1%d" % i) for i in range(4)]
                M["B_celln"] = [Buf("celln0"), Buf("celln1")]
                M["B_mstg"] = Buf("mem_stg")
                M["B_j256"] = Buf("junk256")
                allb += M["B_cell"] + M["B_Cb"] + M["B_sc"] + M["B_vp1"] + M["B_celln"] + [M["B_mstg"], M["B_j256"]]
            M["all"] = allb
            touch(allb, R=[B_arena])
            return M

        slotsM = {}

        def convM(M, s, c0, coltile, tail_idx, t0, outT, B_out):
            upre, ybuf, B_u, B_y = M["upre"], M["ybuf"], M["B_u"], M["B_y"]
            vop(DVE, "tensor_copy", R=[B_tails], W=[B_u], out=upre[:, 0:3], in_=tails[:, tail_idx, :])
            for tt in range(HT // 512):
                bk = next_pj()
                for k in range(16):
                    mm(banks[bk][:, :], wsl_t[s][:, k, c0:c0 + 128], R0[:, k, t0 + tt * 512:t0 + (tt + 1) * 512],
                       k == 0, k == 15, bk, R=[B_w[s], B_R0])
                yield
                cp(ACT, upre[:, 3 + tt * 512:3 + (tt + 1) * 512], banks[bk][:, :], W=BK(bk), J=[B_u])
            vop(DVE, "tensor_copy", R=[B_u], W=[B_tails], out=tails[:, tail_idx, :], in_=upre[:, HT:HT + 3])
            vop(DVE, "tensor_scalar", R=[B_u] + CONSTS, W=[B_y], out=ybuf, in0=upre[:, 0:HT],
                scalar1=cw[:, coltile * 4:coltile * 4 + 1], scalar2=cb[:, coltile:coltile + 1], op0=ALU.mult, op1=ALU.add)
            for tp in range(1, 4):
                vop(DVE, "scalar_tensor_tensor", R=[B_u] + CONSTS, W=[B_y], out=ybuf, in0=upre[:, tp:tp + HT],
                    scalar=cw[:, coltile * 4 + tp:coltile * 4 + tp + 1], in1=ybuf, op0=ALU.mult, op1=ALU.add)
            act(outT, ybuf, AF.Silu, R=[B_y], W=[B_out])
            yield

        def projM(M, h, hf, bs):
            prefix = M["prefix"]
            t0 = hf * HT
            if hf == 0 and h == 0:
                if prefix:
                    slotsM[0] = (load_w("B1_0", slot=0), None)
                else:
                    slotsM[0] = (load_w("B1_0", slot=0), load_w("B2_0", slot=1))
            if prefix and hf == 0 and h + 1 < 8:
                slotsM[h + 1] = (load_w("B1_%d" % (h + 1), slot=(h + 1) % 2), None)
            s, s2 = slotsM[h]
            ktok3 = bs["ktok"].rearrange("p (j c) -> p j c", c=128)
            vaug3 = bs["vaug"][:, 0:HJ * 257].rearrange("p (j c) -> p j c", c=257)
            yield
            yield from convM(M, s, 128, 8 + h, 8 + h, t0, bs["kT"], bs["B_kT"])
            if not prefix:
                yield from convM(M, s, 0, h, h, t0, bs["qT"], bs["B_qT"])
            elif hf == 1:
                bk = next_pj()
                for k in range(16):
                    mm(banks[bk][:, 0:128], wsl_t[s][:, k, 0:128], R0[:, k, 1920:2048], k == 0, k == 15, bk, R=[B_w[s], B_R0])
                vop(DVE, "tensor_copy", W=BK(bk) + [B_tails], out=tails[:, h, :], in_=banks[bk][:, 125:128])
                yield
            ones_src = flag[:, 0:1] if prefix else onesf[:, 0:1]
            for j2 in range(HJ // 2):
                bk = next_pj()
                for i in range(2):
                    jt = hf * HJ + 2 * j2 + i
                    for k in range(16):
                        mm(banks[bk][:, i * 256:(i + 1) * 256], R0[:, k, jt * 128:(jt + 1) * 128], wsl_t[s][:, k, 256:512],
                           k == 0, k == 15, bk, R=[B_w[s], B_R0])
                    yield
                cp(ACT, vaug3[:, 2 * j2:2 * j2 + 2, 0:256], banks[bk][:, 0:512].rearrange("p (a b) -> p a b", b=256),
                   W=BK(bk) + ([bs["B_vaug"]] if j2 == 0 else []), J=[] if j2 == 0 else [bs["B_vaug"]])
            for jj in range(HJ):
                vop(DVE, "tensor_copy", R=CONSTS, J=[bs["B_vaug"]], out=vaug3[:, jj, 256:257], in_=ones_src)
            if (not prefix) and hf == 1 and h + 1 < 8:
                slotsM[h + 1] = (load_w("B1_%d" % (h + 1), slot=0), None)
            for q in range(HJ // 4):
                bk = next_pj()
                for i in range(4):
                    jj = 4 * q + i
                    tr(banks_b[bk][:, i * 128:(i + 1) * 128], bs["kT"][:, jj * 128:(jj + 1) * 128], bk, R=[bs["B_kT"]])
                cp(ACT, ktok3[:, 4 * q:4 * q + 4, :], banks_b[bk][:, 0:512].rearrange("p (a b) -> p a b", b=128),
                   W=BK(bk) + ([bs["B_ktok"]] if q == 0 else []), J=[] if q == 0 else [bs["B_ktok"]])
                yield
            if prefix:
                return
            sigo3 = bs["sigo"].rearrange("p (j c) -> p j c", c=256)
            zT3 = bs["zT"].rearrange("p (a t) -> p a t", a=2)
            for j2 in range(HJ // 2):
                bk = next_pj()
                for i in range(2):
                    jt = hf * HJ + 2 * j2 + i
                    for k in range(16):
                        mm(banks[bk][:, i * 256:(i + 1) * 256], R0[:, k, jt * 128:(jt + 1) * 128], wsl_t[s2][:, k, 0:256],
                           k == 0, k == 15, bk, R=[B_w[s2], B_R0])
                    yield
                cp(ACT, sigo3[:, 2 * j2:2 * j2 + 2, :], banks[bk][:, 0:512].rearrange("p (a b) -> p a b", b=256),
                   W=BK(bk) + ([bs["B_sigo"]] if j2 == 0 else []), J=[] if j2 == 0 else [bs["B_sigo"]])
            for i in range(2):
                for tt in range(HT // 512):
                    bk = next_pj()
                    for k in range(16):
                        mm(banks[bk][:, :], wsl_t[s2][:, k, 256 + 128 * i:384 + 128 * i], R0[:, k, t0 + tt * 512:t0 + (tt + 1) * 512],
                           k == 0, k == 15, bk, R=[B_w[s2], B_R0])
                    yield
                    first = (i == 0 and tt == 0)
                    cp(ACT, zT3[:, i, tt * 512:(tt + 1) * 512], banks[bk][:, :],
                       W=BK(bk) + ([bs["B_zT"]] if first else []), J=[] if first else [bs["B_zT"]])
            if hf == 1 and h + 1 < 8:
                slotsM[h + 1] = (slotsM[h + 1][0], load_w("B2_%d" % (h + 1), slot=1))
            act(bs["zT"], bs["zT"], AF.Silu, R=[bs["B_zT"]], W=[bs["B_zT"]])
            act(bs["sigo"], bs["sigo"], AF.Sigmoid, R=[bs["B_sigo"]], W=[bs["B_sigo"]])
            yield

        def recM(M, h, hf, bs):
            prefix = M["prefix"]
            gs, B_gs = M["gs"], M["B_gs"]
            vpp, B_vpp = M["vpp"], M["B_vpp"]
            ktok3 = bs["ktok"].rearrange("p (j c) -> p j c", c=128)
            vaug3 = bs["vaug"][:, 0:HJ * 257].rearrange("p (j c) -> p j c", c=257)
            t0 = hf * HT
            if not prefix:
                Cb3 = M["Cb"].rearrange("p (j c) -> p j c", c=258)
                sigo3 = bs["sigo"].rearrange("p (j c) -> p j c", c=256)
                zT3 = bs["zT"].rearrange("p (a t) -> p a t", a=2)
                mem3 = M["mem_stg"].rearrange("p (a t) -> p a t", a=2)
                cp(ACT, Cb3[:, 0, 0:257], cstate[:, h, :], R=[B_cstate[h]], W=[M["B_Cb"][0]])

            TBs = (2, 3)
            B_sst = M.setdefault("B_sst", [Buf("sst0"), Buf("sst1")])

            def need_step(jj):
                return prefix or not (hf == 1 and jj == HJ - 1)

            def st0(jj):
                j = hf * HJ + jj
                sl = jj % 2
                col = j * 8 + h
                if need_step(jj):
                    vop(DVE, "tensor_scalar", R=[bs["B_vaug"], B_gs], W=[B_vpp[sl]], out=vpp[sl][:, 0:257], in0=vaug3[:, jj, :],
                        scalar1=gs["w"][:, col:col + 1], scalar2=None, op0=ALU.mult)
                if prefix:
                    return
                obk = 6 + sl
                s4 = jj % 4
                mm(banks[obk][:, 384:512], bs["kT"][:, jj * 128:(jj + 1) * 128], bs["qT"][:, jj * 128:(jj + 1) * 128], True, True, obk,
                   R=[bs["B_kT"], bs["B_qT"]])
                vop(DVE, "tensor_scalar", R=[bs["B_vaug"], B_gs], W=[M["B_vp1"][s4]], out=M["vp1"][s4][:, 0:257], in0=vaug3[:, jj, :],
                    scalar1=gs["u"][:, col:col + 1], scalar2=None, op0=ALU.mult)

            def st1(jj):
                sl = jj % 2
                if need_step(jj):
                    bk = 4 + sl
                    mm(banks[bk][:, 0:257], ktok3[:, jj, :], vpp[sl][:, 0:257], True, True, bk, R=[bs["B_ktok"], B_vpp[sl]])
                if prefix:
                    return
                obk = 6 + sl
                vop(DVE, "tensor_tensor", R=CONSTS, W=BK(obk) + [M["B_sc"][sl]], out=M["scT"][sl], in0=banks[obk][:, 384:512],
                    in1=m01f, op=ALU.mult)

            def st2(jj):
                j = hf * HJ + jj
                sl = jj % 2
                col = j * 8 + h
                if need_step(jj):
                    bk = 4 + sl
                    vop(DVE, "scalar_tensor_tensor", R=[B_gs], W=BK(bk) + [B_cstate[h]], out=cstate[:, h, :],
                        in0=cstate[:, h, :], scalar=gs["ec"][:, col:col + 1], in1=banks[bk][:, 0:257], op0=ALU.mult, op1=ALU.add)
                    if not prefix:
                        cp(DVE, Cb3[:, jj + 1, 0:257], cstate[:, h, :], R=[B_cstate[h]], J=[M["B_Cb"][(jj + 1) % 2]])
                if prefix:
                    return
                obk = 6 + sl
                s4 = jj % 4
                Ops_ = banks[obk][:, 0:257]
                mm(Ops_, M["scT"][sl], M["vp1"][s4][:, 0:257], True, False, obk, R=[M["B_sc"][sl], M["B_vp1"][s4]])
                mm(Ops_, bs["qT"][:, jj * 128:(jj + 1) * 128], Cb3[:, jj, 0:257], False, True, obk, R=[bs["B_qT"], M["B_Cb"][jj % 2]])

            def st3(jj):
                j = hf * HJ + jj
                sl = jj % 2
                col = j * 8 + h
                obk = 6 + sl
                sc_ = stat[:, 48 + 4 * sl:52 + 4 * sl]
                vop(DVE, "tensor_scalar", W=BK(obk) + [B_sst[sl]], out=sc_[:, 0:1], in0=banks[obk][:, 256:257],
                    scalar1=-1.0, scalar2=None, op0=ALU.mult)
                vop(DVE, "tensor_scalar", R=[B_gs], W=BK(obk) + [B_sst[sl]], out=sc_[:, 1:2], in0=banks[obk][:, 256:257],
                    scalar1=sc_[:, 0:1], scalar2=gs["emb"][:, col:col + 1], op0=ALU.max, op1=ALU.max)
                vop(DVE, "reciprocal", R=[B_sst[sl]], W=[B_sst[sl]], out=sc_[:, 0:1], in_=sc_[:, 1:2])
                vop(DVE, "scalar_tensor_tensor", R=[B_sst[sl], bs["B_sigo"]], W=BK(obk) + [M["B_cell"][sl]], out=M["cellf"][sl],
                    in0=banks[obk][:, 0:256], scalar=sc_[:, 0:1], in1=sigo3[:, jj, :], op0=ALU.mult, op1=ALU.mult)

            def st4(jj):
                sl = jj % 2
                sc_ = stat[:, 48 + 4 * sl:52 + 4 * sl]
                act(M["junk256"], M["cellf"][sl], AF.Square, R=[M["B_cell"][sl]], W=[M["B_j256"], B_sst[sl]], accum_out=sc_[:, 2:3])
                act(sc_[:, 2:3], sc_[:, 2:3], AF.Sqrt, R=[B_sst[sl]], W=[B_sst[sl]], scale=1.0 / 256.0, bias=EPS)

            def st5(jj):
                sl = jj % 2
                sc_ = stat[:, 48 + 4 * sl:52 + 4 * sl]
                vop(DVE, "reciprocal", R=[B_sst[sl]], W=[B_sst[sl]], out=sc_[:, 3:4], in_=sc_[:, 2:3])
                vop(DVE, "tensor_scalar", R=[B_sst[sl], M["B_cell"][sl]], W=[M["B_celln"][sl]], out=M["celln"][sl], in0=M["cellf"][sl],
                    scalar1=sc_[:, 3:4], scalar2=None, op0=ALU.mult)

            def st6(jj):
                sl = jj % 2
                TB = TBs[sl]
                for i in range(2):
                    tr(banks_b[TB][:, i * 128:(i + 1) * 128], M["celln"][sl][:, i * 128:(i + 1) * 128], TB, R=[M["B_celln"][sl]])

            def st7(jj):
                sl = jj % 2
                TB = TBs[sl]
                for i in range(2):
                    first = (jj == 0 and i == 0)
                    vop(DVE, "scalar_tensor_tensor", R=[bs["B_zT"]] + CONSTS,
                        W=BK(TB) + ([M["B_mstg"]] if first else []), J=[] if first else [M["B_mstg"]],
                        out=mem3[:, i, jj * 128:(jj + 1) * 128], in0=banks_b[TB][:, i * 128:(i + 1) * 128],
                        scalar=gm_fm[:, 2 * h + i:2 * h + i + 1], in1=zT3[:, i, jj * 128:(jj + 1) * 128], op0=ALU.mult, op1=ALU.mult)

            stages = [st0, st1, st2] if prefix else [st0, st1, st2, st3, st4, st5, st6, st7]
            ns = len(stages)
            yield
            for it in range(HJ + ns - 1):
                for k in range(ns - 1, -1, -1):
                    jj = it - k
                    if 0 <= jj < HJ:
                        stages[k](jj)
                yield
            if not prefix:
                for i in range(2):
                    dma(SP, s_mem[2 * h + i][:, t0:t0 + HT], mem3[:, i, :], "mstg", R=[M["B_mstg"]], J=[B_smem])
                yield

        def run_units(M, nheads):
            units = [(h, hf) for h in range(nheads) for hf in range(2)]
            pj_set[:] = [0, 1]
            interleave(projM(M, units[0][0], units[0][1], M["sets"][0]))
            for ui, (h, hf) in enumerate(units):
                grec = recM(M, h, hf, M["sets"][ui % 2])
                gproj = None
                if ui + 1 < len(units):
                    h2, hf2 = units[ui + 1]
                    gproj = projM(M, h2, hf2, M["sets"][(ui + 1) % 2])
                while grec is not None or gproj is not None:
                    if grec is not None:
                        try:
                            next(grec)
                        except StopIteration:
                            grec = None
                    for _ in range(2):
                        if gproj is not None:
                            try:
                                next(gproj)
                            except StopIteration:
                                gproj = None
            pj_set[:] = [0, 1, 2, 3]

        arena_users = phase_norm(x_pre)
        fence(arena_users)
        carve.reset()
        stg = [carve.b16(2048) for _ in range(4)]
        B_stg = [Buf("stg%d" % i) for i in range(4)]
        B_skv = Buf("s_kv")
        touch(B_stg, R=[B_arena])
        MP = carve_mlstm(True)
        gate_math(MP["gs"], MP["B_gs"])
        for h in range(16):
            s = load_w("A%d" % h)
            for which, (c0, dst) in enumerate(((128, s_kT), (256, s_vT))):
                si = (2 * h + which) % 4

                def ev(tt, bk, si=si):
                    cp(ACT if tt % 2 == 0 else DVE, stg[si][:, tt * 512:(tt + 1) * 512], banks[bk][:, :],
                       W=BK(bk) + ([B_stg[si]] if tt == 0 else []), J=[] if tt == 0 else [B_stg[si]])
                proj_fm(s, c0, ev)
                dma(SP, dst[h], stg[si], "stg%d" % si, R=[B_stg[si]], J=[B_skv])
        run_units(MP, 8)
        fence(B_stg + MP["all"])


        arena_users = phase_norm(x_own)
        fence(arena_users)

        carve.reset()
        setsA = []
        for i_ in range(2):
            setsA.append(dict(qT=carve.b16(2048), kT=carve.b16(4096), vT=carve.b16(4096), zT=carve.b16(2048),
                              B_qT=Buf("qT%d" % i_), B_kT=Buf("kT%d" % i_), B_vT=Buf("vT%d" % i_), B_zT=Buf("zT%d" % i_), i=i_))
        NVT = 69
        Vt = carve.b16(NVT * 128)
        Pt = [carve.b16(256) for _ in range(2)]
        attn_stg = carve.b16(2048)
        biasT = [carve.f32(256) for _ in range(3)]
        tmpS = [carve.f32(256) for _ in range(2)]
        acc = carve.f32(4096)
        acc3 = acc.rearrange("p (a t) -> p a t", a=2)
        Otmp = [carve.f32(256) for _ in range(2)]
        B_ot = [Buf("Otmp0"), Buf("Otmp1")]
        B_Vt, B_astg, B_bias, B_acc = Buf("Vt"), Buf("attn_stg"), Buf("biasT"), Buf("acc")
        B_Pt = [Buf("Pt%d" % i) for i in range(2)]
        B_tmpS = [Buf("tmpS%d" % i) for i in range(2)]
        allA = [B_Vt, B_astg, B_bias, B_acc] + B_Pt + B_tmpS + B_ot
        for st_ in setsA:
            allA += [st_["B_qT"], st_["B_kT"], st_["B_vT"], st_["B_zT"]]
        touch(allA, R=[B_arena])
        B_sattn = Buf("s_attn")

        vt_base = {1: 0, 4: 17, 16: 37}

        def vt_idx(r, blk, ph):
            return vt_base[r] + blk * r + ph

        def projA(h, bs):
            qT, kT_all, vT_all, zT = bs["qT"], bs["kT"], bs["vT"], bs["zT"]
            B_qT, B_kT, B_vT, B_zT = bs["B_qT"], bs["B_kT"], bs["B_vT"], bs["B_zT"]
            s = slotA[h]
            dma(SP, kT_all[:, 0:2048], s_kT[h], "kTl%d" % bs["i"], R=[B_skv], W=[B_kT])
            dma(SP, vT_all[:, 0:2048], s_vT[h], "vTl%d" % bs["i"], R=[B_skv], W=[B_vT])
            yield

            def ev_q(tt, bk):
                act(qT[:, tt * 512:(tt + 1) * 512], banks[bk][:, :], AF.Copy,
                    W=BK(bk) + ([B_qT] if tt == 0 else []), J=[] if tt == 0 else [B_qT], scale=QSCALE)

            def ev_k(tt, bk):
                cp(ACT, kT_all[:, 2048 + tt * 512:2048 + (tt + 1) * 512], banks[bk][:, :], W=BK(bk), J=[B_kT])

            def ev_v(tt, bk):
                cp(ACT, vT_all[:, 2048 + tt * 512:2048 + (tt + 1) * 512], banks[bk][:, :], W=BK(bk), J=[B_vT])

            def ev_z(tt, bk):
                cp(ACT, zT[:, tt * 512:(tt + 1) * 512], banks[bk][:, :],
                   W=BK(bk) + ([B_zT] if tt == 0 else []), J=[] if tt == 0 else [B_zT])
            pend = []
            for c0, ev in ((256, ev_v), (128, ev_k), (0, ev_q), (384, ev_z)):
                for tt in range(4):
                    bk = next_pj()
                    for k in range(16):
                        mm(banks[bk][:, :], wsl_t[s][:, k, c0:c0 + 128], R0[:, k, tt * 512:(tt + 1) * 512],
                           k == 0, k == 15, bk, R=[B_w[s], B_R0])
                        if k % 2 == 1:
                            yield
                            if k == 3 and pend:
                                pend.pop()()
                    pend.append(lambda ev=ev, tt=tt, bk=bk: ev(tt, bk))
            while pend:
                pend.pop()()

        def attnA(h, bs):
            qT, kT_all, vT_all, zT = bs["qT"], bs["kT"], bs["vT"], bs["zT"]
            B_qT, B_kT, B_vT, B_zT = bs["B_qT"], bs["B_kT"], bs["B_vT"], bs["B_zT"]
            tiles = []
            for r in (1, 4, 16):
                span = 128 * r
                nsp = 2048 // span
                for blk in range(nsp + 1):
                    for ph in range(r):
                        tiles.append((vt_idx(r, blk, ph), 2048 + (blk - 1) * span + ph, r))
            TB = 3
            act(zT, zT, AF.Silu, R=[B_zT], W=[B_zT])
            for g0 in range(0, len(tiles), 4):
                grp_ = tiles[g0:g0 + 4]
                for i, (vi, start, r) in enumerate(grp_):
                    tr(banks_b[TB][:, i * 128:(i + 1) * 128], vT_all[:, start:start + 127 * r + 1:r], TB, R=[B_vT])
                vi0 = grp_[0][0]
                n = len(grp_)
                assert all(grp_[i][0] == vi0 + i for i in range(n))
                cp(ACT, Vt[:, vi0 * 128:(vi0 + n) * 128], banks_b[TB][:, 0:n * 128],
                   W=BK(TB) + ([B_Vt] if g0 == 0 else []), J=[] if g0 == 0 else [B_Vt])
                yield (2 if (g0 // 4) % 2 == 0 else 1)
            slope = 2.0 ** (-(h + 1) / 2.0)
            for pi, r in enumerate((1, 4, 16)):
                vop(DVE, "scalar_tensor_tensor", R=CONSTS, W=[B_bias] if pi == 0 else [], J=[] if pi == 0 else [B_bias],
                    out=biasT[pi], in0=distPC, scalar=float(-slope * r), in1=maskPC, op0=ALU.mult, op1=ALU.add)
            tl = []
            for pi, r in enumerate((1, 4, 16)):
                span = 128 * r
                for a in range(2048 // span):
                    for ph in range(r):
                        tl.append((pi, r, a, ph))
            state = {"first_acc": True}

            def emit_S(ti):
                pi, r, a, ph = tl[ti]
                span = 128 * r
                qs = a * span + ph
                pcur = 2048 + qs
                pprev = pcur - span
                sl = ti % 2
                sbk = 4 + sl
                Sps = banks[sbk][:, 0:256]
                qsl = qT[:, qs:qs + 127 * r + 1:r]
                mm(Sps[:, 0:128], kT_all[:, pprev:pprev + 127 * r + 1:r], qsl, True, True, sbk, R=[B_kT, B_qT])
                mm(Sps[:, 128:256], kT_all[:, pcur:pcur + 127 * r + 1:r], qsl, True, True, sbk, R=[B_kT, B_qT])
                vop(DVE, "tensor_tensor", R=[B_bias], W=BK(sbk) + [B_tmpS[sl]], out=tmpS[sl], in0=Sps, in1=biasT[pi], op=ALU.add)
                act(Pt[sl], tmpS[sl], AF.Exp, R=[B_tmpS[sl]], W=[B_Pt[sl]])

            def emit_PV(ti):
                pi, r, a, ph = tl[ti]
                span = 128 * r
                qs = a * span + ph
                sl = ti % 2
                obk = 6 + sl
                Ops_ = banks[obk][:, 0:256]
                vprev = vt_idx(r, a, ph)
                vcur = vt_idx(r, a + 1, ph)
                mm(Ops_[:, 0:128], Vt[:, vprev * 128:(vprev + 1) * 128], Pt[sl][:, 0:128], True, False, obk, R=[B_Vt, B_Pt[sl]])
                mm(Ops_[:, 0:128], Vt[:, vcur * 128:(vcur + 1) * 128], Pt[sl][:, 128:256], False, True, obk, R=[B_Vt, B_Pt[sl]])
                ones_prev = flagones if a == 0 else onesb
                mm(Ops_[:, 128:256], ones_prev[:], Pt[sl][:, 0:128], True, False, obk, R=[B_Pt[sl]] + CONSTS)
                mm(Ops_[:, 128:256], onesb[:], Pt[sl][:, 128:256], False, True, obk, R=[B_Pt[sl]] + CONSTS)
                dst = acc3[:, :, qs:qs + 127 * r + 1:r]
                src = Ops_.rearrange("p (a t) -> p a t", a=2)
                if pi == 0:
                    fa = state["first_acc"]
                    cp(ACT, dst, src, W=BK(obk) + ([B_acc] if fa else []), J=[] if fa else [B_acc])
                    state["first_acc"] = False
                else:
                    cp(ACT, Otmp[sl], Ops_, W=BK(obk) + [B_ot[sl]])
                    vop(POOL, "tensor_tensor", R=[B_ot[sl], B_acc], J=[B_acc], out=dst,
                        in0=Otmp[sl].rearrange("p (a t) -> p a t", a=2), in1=dst, op=ALU.add)

            def emit_FINa(ti):
                if not (0 <= ti < len(tl)):
                    return
                pi, r, a, ph = tl[ti]
                span = 128 * r
                qs = a * span + ph
                if pi == 2:
                    qsl_ = slice(qs, qs + 127 * r + 1, r)
                    vop(DVE, "reciprocal", R=[B_acc], J=[B_acc], out=acc3[:, 1, qsl_], in_=acc3[:, 1, qsl_])
                    vop(DVE, "tensor_tensor", R=[B_acc], J=[B_acc], out=acc3[:, 0, qsl_], in0=acc3[:, 0, qsl_],
                        in1=acc3[:, 1, qsl_], op=ALU.mult)

            def emit_FINb(ti):
                if not (0 <= ti < len(tl)):
                    return
                pi, r, a, ph = tl[ti]
                span = 128 * r
                qs = a * span + ph
                if pi == 2:
                    qsl_ = slice(qs, qs + 127 * r + 1, r)
                    fs_ = state.get("first_stg", True)
                    vop(POOL, "tensor_tensor", R=[B_acc, B_zT], W=[B_astg] if fs_ else [], J=[] if fs_ else [B_astg],
                        out=attn_stg[:, qsl_], in0=acc3[:, 0, qsl_], in1=zT[:, qsl_], op=ALU.mult)
                    state["first_stg"] = False

            NTL = len(tl)
            emit_S(0)
            yield 1
            emit_S(1)
            yield 2
            for ti in range(1, NTL + 1):
                emit_FINb(ti - 5)
                emit_FINa(ti - 4)
                emit_PV(ti - 1)
                if ti + 1 < NTL:
                    emit_S(ti + 1)
                if tl[min(ti, NTL - 1)][0] == 2:
                    yield 3
                else:
                    yield (2 if ti % 2 == 0 else 1)
            emit_FINb(NTL - 4)
            for ti in range(NTL - 3, NTL):
                emit_FINa(ti)
            yield 1
            for ti in range(NTL - 3, NTL):
                emit_FINb(ti)
            yield 1
            dma(SP, s_attn[h], attn_stg, "astg", R=[B_astg], J=[B_sattn])
            yield 1

        def interleave(*gens):
            gens = list(gens)
            while gens:
                for g in list(gens):
                    try:
                        next(g)
                    except StopIteration:
                        gens.remove(g)

        pj_set[:] = [0, 1, 2]
        slotA = {0: load_w("A0"), 1: load_w("A1")}
        interleave(projA(0, setsA[0]))
        for h in range(16):
            if h + 2 < 16:
                slotA[h + 2] = load_w("A%d" % (h + 2))
            gattn = attnA(h, setsA[h % 2])
            gproj = projA(h + 1, setsA[(h + 1) % 2]) if h + 1 < 16 else None
            while gattn is not None or gproj is not None:
                npull = 4
                if gattn is not None:
                    try:
                        npull = next(gattn) or 1
                    except StopIteration:
                        gattn = None
                for _ in range(npull):
                    if gproj is not None:
                        try:
                            next(gproj)
                        except StopIteration:
                            gproj = None
        pj_set[:] = [0, 1, 2, 3]
        fence(allA)

        carve.reset()
        MB = carve_mlstm(False)
        gate_math(MB["gs"], MB["B_gs"])
        wload_rr[0] = 0
        run_units(MB, 8)
        fence(MB["all"])


        carve.reset()
        mergedT = carve.b16(16 * 2048)
        merged3 = mergedT.rearrange("p (k t) -> p k t", k=16)
        gst = [carve.b16(2048) for _ in range(2)]
        tmpm = [carve.b16(512) for _ in range(2)]
        B_merged = Buf("mergedT")
        B_gst = [Buf("gst0"), Buf("gst1")]
        B_tmpm = [Buf("tmpm0"), Buf("tmpm1")]
        allC = [B_merged] + B_gst + B_tmpm
        touch(allC, R=[B_arena])
        B_sg = Buf("s_g")
        gi = 0
        for nm, dst in (("GA", s_ga), ("GM", s_gm)):
            for cbk in range(4):
                s = load_w("%s%d" % (nm, cbk))
                for ct in range(4):
                    dcol = cbk * 4 + ct
                    sl = gi % 2
                    gi += 1

                    def ev_g(tt, bk, sl=sl):
                        act(gst[sl][:, tt * 512:(tt + 1) * 512], banks[bk][:, :], AF.Sigmoid,
                            W=BK(bk) + ([B_gst[sl]] if tt == 0 else []), J=[] if tt == 0 else [B_gst[sl]])
                    proj_fm(s, ct * 128, ev_g)
                    dma(SP, dst[dcol], gst[sl], "gst%d" % sl, R=[B_gst[sl]], J=[B_sg])

        for branch, (nm, src, Bsrc, gsrc) in enumerate((("WA", s_attn, B_sattn, s_ga), ("WM", s_mem, B_smem, s_gm))):
            for k in range(16):
                dma(SP, R0[:, k, :], src[k], "R0l", R=[Bsrc], W=[B_R0] if k == 0 else [], J=[] if k == 0 else [B_R0])
            for cbk in range(4):
                s = load_w("%s%d" % (nm, cbk))
                for ct in range(4):
                    dcol = cbk * 4 + ct
                    sl = gi % 2
                    gi += 1
                    dma(SP, gst[sl], gsrc[dcol], "gld%d" % sl, R=[B_sg], W=[B_gst[sl]])

                    def ev_y(tt, bk, sl=sl, dcol=dcol, branch=branch):
                        dst = merged3[:, dcol, tt * 512:(tt + 1) * 512]
                        if branch == 0:
                            vop(DVE, "tensor_tensor", R=[B_gst[sl]], W=BK(bk), J=[B_merged], out=dst, in0=banks[bk][:, :],
                                in1=gst[sl][:, tt * 512:(tt + 1) * 512], op=ALU.mult)
                        else:
                            ts_ = tt % 2
                            vop(DVE, "tensor_tensor", R=[B_gst[sl]], W=BK(bk) + [B_tmpm[ts_]], out=tmpm[ts_], in0=banks[bk][:, :],
                                in1=gst[sl][:, tt * 512:(tt + 1) * 512], op=ALU.mult)
                            vop(DVE, "tensor_tensor", R=[B_tmpm[ts_], B_merged], J=[B_merged], out=dst, in0=dst, in1=tmpm[ts_], op=ALU.add)
                    proj_fm(s, ct * 128, ev_y)

        for cbk in range(4):
            off, c = _BLK_OFF["WO%d" % cbk]
            srcw = wst[:, off:off + 16 * c].rearrange("p (k c) -> p k c", c=c)
            dma(POOL, R0[:, :, cbk * 512:(cbk + 1) * 512], srcw, "R0w", W=[B_R0] if cbk == 0 else [], J=[] if cbk == 0 else [B_R0],
                max_dma_last_dim=8192)
        w0f = wsl_t[0].bitcast(F32)
        w1f = wsl_t[1].bitcast(F32)
        res = [w0f[:, 0:8, :].rearrange("p a b -> p (a b)"), w0f[:, 8:16, :].rearrange("p a b -> p (a b)")]
        fg_rep = w1f[:, 0:8, :].rearrange("p a b -> p (a b)")
        junkf = wsl_t[1][:, 8:12, :].rearrange("p a b -> p (a b)")
        B_res = [Buf("res0"), Buf("res1")]
        B_fg = Buf("fg_rep")
        B_junkf = Buf("junkf")
        B_outs = Buf("outs")
        touch([B_w[0], B_res[0], B_res[1]], R=[B_w[0]])
        touch([B_w[1], B_fg, B_junkf], R=[B_w[1]])
        dma(SP, fg_rep, fg_rep_d, "fg", W=[B_fg])
        for j in range(NT):
            sl = j % 2
            dma(SP, res[sl], x_own[j * 128:(j + 1) * 128, :], "res%d" % sl, W=[B_res[sl]])
            for cg in range(4):
                bk = next_pj()
                for k in range(16):
                    mm(banks[bk][:, :], merged3[:, k, j * 128:(j + 1) * 128], R0[:, k, cg * 512:(cg + 1) * 512], k == 0, k == 15,
                       bk, R=[B_merged, B_R0])
                vop(DVE, "tensor_tensor", R=[B_res[sl]], W=BK(bk), J=[B_res[sl]], out=res[sl][:, cg * 512:(cg + 1) * 512],
                    in0=banks[bk][:, :], in1=res[sl][:, cg * 512:(cg + 1) * 512], op=ALU.add)
            sc_ = stat[:, 56 + 3 * sl:59 + 3 * sl]
            act(junkf, res[sl], AF.Square, R=[B_res[sl]], W=[B_junkf], J=[B_stat], accum_out=sc_[:, 0:1])
            act(sc_[:, 1:2], sc_[:, 0:1], AF.Sqrt, R=[B_stat], J=[B_stat], scale=1.0 / D, bias=EPS)
            vop(DVE, "reciprocal", R=[B_stat], J=[B_stat], out=sc_[:, 2:3], in_=sc_[:, 1:2])
            vop(DVE, "scalar_tensor_tensor", R=[B_stat, B_fg], W=[B_res[sl]], out=res[sl], in0=res[sl],
                scalar=sc_[:, 2:3], in1=fg_rep, op0=ALU.mult, op1=ALU.mult)
            o = dma(SP, out_d[j * 128:(j + 1) * 128, :], res[sl], "out%d" % sl, R=[B_res[sl]], J=[B_outs])
            P.final_waits.append(o)
        if dbg:
            P.final_waits.extend(B_sattn.writers)
            P.final_waits.extend(B_smem.writers)

        sems = []

        def sem_alloc(name):
            sm = st.enter_context(nc.semaphore(name))
            sems.append(sm)
            return sm
        P.emit(sem_alloc)
        nc._n_ops = {e: len(P.ops[e]) for e in ENGS}
        nc._n_sems = len(sems)
    return nc


_CACHE = {}


def _prep_inputs(x, norm_g, w_in, b_if, conv_w, conv_b, mlstm_norm_g, w_attn_branch, w_mlstm_branch, w_out, final_norm_g):
    f32 = np.float32
    x = np.asarray(x, f32)
    wstream = _build_wstream(np.asarray(w_in, f32), np.asarray(w_attn_branch, f32), np.asarray(w_mlstm_branch, f32), np.asarray(w_out, f32))
    g_rep = np.ascontiguousarray(np.broadcast_to(np.asarray(norm_g, f32)[None, :], (128, D)))
    fg_rep = np.ascontiguousarray(np.broadcast_to(np.asarray(final_norm_g, f32)[None, :], (128, D)))
    gm_fm = np.ascontiguousarray(np.asarray(mlstm_norm_g, f32).reshape(16, 128).T)
    cwt = np.asarray(conv_w, f32)
    cw = np.ascontiguousarray(cwt.reshape(4, 16, 128).transpose(2, 1, 0).reshape(128, 64))
    cb = np.ascontiguousarray(np.asarray(conv_b, f32).reshape(16, 128).T)
    bif = np.ascontiguousarray(np.broadcast_to(np.tile(np.asarray(b_if, f32), 16)[None, :], (128, 256)))
    in_maps = []
    zeros = np.zeros((TOK, D), f32)
    for c in range(8):
        b, hh = c // 2, c % 2
        flag = np.full((128, 2), float(hh), f32)
        in_maps.append({
            "x_own": np.ascontiguousarray(x[b, hh * TOK:(hh + 1) * TOK]),
            "x_pre": np.ascontiguousarray(x[b, 0:TOK]) if hh == 1 else zeros,
            "wst": wstream, "g_rep": g_rep, "fg_rep": fg_rep, "gm_fm": gm_fm, "cw": cw, "cb": cb, "bif": bif, "flag": flag,
        })
    return in_maps


def kernel(x, norm_g, w_in, b_if, conv_w, conv_b, mlstm_norm_g, w_attn_branch, w_mlstm_branch, w_out, final_norm_g):
    in_maps = _prep_inputs(x, norm_g, w_in, b_if, conv_w, conv_b, mlstm_norm_g, w_attn_branch, w_mlstm_branch, w_out, final_norm_g)
    if "nc" not in _CACHE:
        _CACHE["nc"] = build_nc()
    nc = _CACHE["nc"]
    res = run_bass_kernel_spmd(nc, in_maps, core_ids=list(range(8)))
    out = np.empty((4, 4096, D), np.float32)
    for c in range(8):
        b, hh = c // 2, c % 2
        out[b, hh * TOK:(hh + 1) * TOK] = res.results[c]["out"]
    return out
```
